# Optimizing a Trainium2 kernel written in Bass

```python
import math
import jax, jax.numpy as jnp
from jax import lax
import numpy as np

D_MODEL = 1024
BATCH = 8
SEQ = 4096
DEPTH = 2
DEC_BATCH = 16
DEC_SEQ = 32
PAST_LEN = 1024

CHUNK = 64
POOL_WINDOWS = (2, 4, 8, 16)
N_POOL_GROUPS = 4
POOL_WIDTH = D_MODEL
POOL_GROUP = POOL_WIDTH // N_POOL_GROUPS
POOL_STATE = 15
HEAD_DK = 128
HEAD_DV = 128
N_HEADS = D_MODEL // HEAD_DV
QK_WIDTH = N_HEADS * HEAD_DK
V_WIDTH = N_HEADS * HEAD_DV
CONV_WIDTH = 4
CONV_CH = 2 * QK_WIDTH + V_WIDTH
D_FF = 4 * D_MODEL
ALPHA = (2 * DEPTH) ** 0.25
BETA_INIT = (8 * DEPTH) ** -0.25
LN_EPS = 1e-5
RMS_EPS = 1e-6
L2_EPS = 1e-6
OFF_POOL = POOL_WIDTH
OFF_QKV = OFF_POOL + CONV_CH
OFF_Z = OFF_QKV + V_WIDTH
OFF_GA = OFF_Z + D_MODEL
OFF_GB = OFF_GA + D_MODEL
OFF_BETA = OFF_GB + N_HEADS
IN_WIDTH = OFF_BETA + N_HEADS

kernel_name = "hybrid_pool_gdn_streaming_step"


def layer_norm(x, g, b):
    xf = x.astype(jnp.float32)
    mu = jnp.mean(xf, -1, keepdims=True)
    var = jnp.mean(jnp.square(xf - mu), -1, keepdims=True)
    return ((xf - mu) * lax.rsqrt(var + LN_EPS) * g.astype(jnp.float32) + b.astype(jnp.float32)).astype(x.dtype)


def l2_normalize(x):
    xf = x.astype(jnp.float32)
    return xf * lax.rsqrt(jnp.sum(xf * xf, -1, keepdims=True) + L2_EPS)


def pool_mixer(u_ext, pos0, w_pool, pool_scale):
    B, Lx, _ = u_ext.shape
    L = Lx - POOL_STATE
    uf = u_ext.astype(jnp.float32)
    cs = jnp.concatenate([jnp.zeros((B, 1, POOL_WIDTH), jnp.float32), jnp.cumsum(uf, axis=1)], axis=1)
    end = cs[:, POOL_STATE + 1:]
    pos = pos0 + jnp.arange(L)
    means = []
    for gi, w in enumerate(POOL_WINDOWS):
        lo, hi = gi * POOL_GROUP, (gi + 1) * POOL_GROUP
        start = cs[:, POOL_STATE + 1 - w: POOL_STATE + 1 - w + L, lo:hi]
        cnt = jnp.minimum(w, pos + 1).astype(jnp.float32)[None, :, None]
        means.append((end[..., lo:hi] - start) / cnt)
    mixed = (jnp.concatenate(means, -1) - uf[:, POOL_STATE:]).astype(u_ext.dtype)
    mixed = mixed.reshape(B, L, N_POOL_GROUPS, POOL_GROUP)
    y = jnp.einsum('blgc,gcd->blgd', mixed, w_pool).reshape(B, L, POOL_WIDTH)
    return y * pool_scale


def causal_short_conv(x_ext, conv_w):
    L = x_ext.shape[1] - (CONV_WIDTH - 1)
    y = x_ext[:, 0:L] * conv_w[0]
    for t in range(1, CONV_WIDTH):
        y = y + x_ext[:, t:t + L] * conv_w[t]
    return jax.nn.silu(y)


def gated_delta_rule(q, k, v, g, beta, S0):
    B, L, H, DK = q.shape
    DV = v.shape[-1]
    C = min(CHUNK, L)
    n = L // C
    f32 = jnp.float32

    def chunks4(t):
        return jnp.moveaxis(t.astype(f32).reshape(B, n, C, H, t.shape[-1]), 3, 2)

    def chunks3(t):
        return jnp.moveaxis(t.astype(f32).reshape(B, n, C, H), 3, 2)

    qc, kc, vc = chunks4(q), chunks4(k), chunks4(v)
    gc = jnp.cumsum(chunks3(g), axis=-1)
    bc = chunks3(beta)
    idx = jnp.arange(C)
    incl = idx[:, None] >= idx[None, :]
    strict = idx[:, None] > idx[None, :]
    decay = jnp.exp(jnp.where(incl, gc[..., :, None] - gc[..., None, :], -jnp.inf))
    kb = kc * bc[..., None]
    lower = jnp.where(strict, jnp.einsum('bnhid,bnhjd->bnhij', kb, kc) * decay, 0.0)
    a_mat = jnp.eye(C, dtype=f32) + lower
    rhs = jnp.concatenate([vc * bc[..., None], kb * jnp.exp(gc)[..., None]], axis=-1)
    sol = lax.linalg.triangular_solve(a_mat, rhs, left_side=True, lower=True, unit_diagonal=True)
    u_val, w_dec = sol[..., :DV], sol[..., DV:]
    attn_in = jnp.einsum('bnhid,bnhjd->bnhij', qc, kc) * decay
    q_dec = qc * jnp.exp(gc)[..., None]
    k_tail = kc * jnp.exp(gc[..., -1:] - gc)[..., None]
    g_tot = jnp.exp(gc[..., -1])

    def step(S, inp):
        u_i, w_i, a_i, qd_i, kt_i, gt_i = inp
        v_new = u_i - jnp.einsum('bhcd,bhde->bhce', w_i, S)
        o_i = jnp.einsum('bhcd,bhde->bhce', qd_i, S) + jnp.einsum('bhij,bhje->bhie', a_i, v_new)
        S = S * gt_i[..., None, None] + jnp.einsum('bhcd,bhce->bhde', kt_i, v_new)
        return S, o_i

    xs = tuple(jnp.moveaxis(t, 1, 0) for t in (u_val, w_dec, attn_in, q_dec, k_tail, g_tot))
    S_fin, o = lax.scan(step, S0.astype(f32), xs)
    o = jnp.transpose(o, (1, 0, 3, 2, 4)).reshape(B, L, H, DV)
    return o.astype(v.dtype), S_fin.astype(S0.dtype)


def delta_branch(qkv_ext, z, b_raw, a_raw, S0, conv_w, a_log, dt_bias, o_gain):
    B = qkv_ext.shape[0]
    qkv = causal_short_conv(qkv_ext, conv_w)
    L = qkv.shape[1]
    q, k, v = jnp.split(qkv, [QK_WIDTH, 2 * QK_WIDTH], axis=-1)
    q = l2_normalize(q.reshape(B, L, N_HEADS, HEAD_DK)) * (HEAD_DK ** -0.5)
    k = l2_normalize(k.reshape(B, L, N_HEADS, HEAD_DK))
    v = v.reshape(B, L, N_HEADS, HEAD_DV)
    beta = jax.nn.sigmoid(b_raw.astype(jnp.float32))
    g = -jnp.exp(a_log.astype(jnp.float32)) * jax.nn.softplus(a_raw.astype(jnp.float32) + dt_bias.astype(jnp.float32))
    o, S = gated_delta_rule(q, k, v, g, beta, S0)
    of = o.astype(jnp.float32)
    of = of * lax.rsqrt(jnp.mean(of * of, -1, keepdims=True) + RMS_EPS) * o_gain.astype(jnp.float32)
    of = of * jax.nn.silu(z.astype(jnp.float32).reshape(B, L, N_HEADS, HEAD_DV))
    return of.reshape(B, L, V_WIDTH).astype(qkv_ext.dtype), S


def trunk_layer(x, pool_hist, conv_hist, S0, pos0, w_in, conv_w, a_log, dt_bias, o_gain, w_pool, pool_scale,
                w_out, ln1_g, ln1_b, w_ff1, b_ff1, w_ff2, b_ff2, ln2_g, ln2_b):
    proj = jnp.einsum('bld,de->ble', x, w_in)
    u_pool, qkv, z, ga, gb, b_raw, a_raw = jnp.split(proj, [OFF_POOL, OFF_QKV, OFF_Z, OFF_GA, OFF_GB, OFF_BETA], axis=-1)
    pool_ext = jnp.concatenate([pool_hist, u_pool], axis=1)
    conv_ext = jnp.concatenate([conv_hist, qkv], axis=1)
    y_a = pool_mixer(pool_ext, pos0, w_pool, pool_scale)
    y_b, S_new = delta_branch(conv_ext, z, b_raw, a_raw, S0, conv_w, a_log, dt_bias, o_gain)
    merged = jax.nn.sigmoid(ga) * y_a + jax.nn.sigmoid(gb) * y_b
    x = layer_norm(ALPHA * x + jnp.einsum('bld,de->ble', merged, w_out), ln1_g, ln1_b)
    h = jnp.square(jax.nn.relu(jnp.einsum('bld,df->blf', x, w_ff1) + b_ff1))
    x = layer_norm(ALPHA * x + jnp.einsum('blf,fd->bld', h, w_ff2) + b_ff2, ln2_g, ln2_b)
    return x, pool_ext[:, -POOL_STATE:], conv_ext[:, -(CONV_WIDTH - 1):], S_new


def setup_inputs(seed: int = 0) -> dict:
    key = jax.random.key(seed)
    ks = jax.random.split(key, 24)
    f32 = jnp.float32
    nrm = lambda k, shape, s: jax.random.normal(k, shape, f32) * s
    dt = jnp.exp(jax.random.uniform(ks[7], (DEPTH, N_HEADS), f32) * (math.log(0.1) - math.log(0.001)) + math.log(0.001))
    return {
        "x_prompt": nrm(ks[0], (BATCH, SEQ, D_MODEL), 1.0),
        "x_sample": nrm(ks[1], (DEC_BATCH, DEC_SEQ, D_MODEL), 1.0),
        "state_pool": nrm(ks[2], (DEPTH, DEC_BATCH, POOL_STATE, POOL_WIDTH), 1.0),
        "state_conv": nrm(ks[3], (DEPTH, DEC_BATCH, CONV_WIDTH - 1, CONV_CH), 1.0),
        "state_delta": nrm(ks[4], (DEPTH, DEC_BATCH, N_HEADS, HEAD_DK, HEAD_DV), 0.1),
        "ln_in_g": 1.0 + nrm(ks[5], (D_MODEL,), 0.02),
        "ln_in_b": nrm(ks[6], (D_MODEL,), 0.02),
        "w_in": nrm(ks[8], (DEPTH, D_MODEL, IN_WIDTH), D_MODEL ** -0.5),
        "conv_w": nrm(ks[9], (DEPTH, CONV_WIDTH, CONV_CH), CONV_WIDTH ** -0.5),
        "a_log": jnp.log(jax.random.uniform(ks[10], (DEPTH, N_HEADS), f32, 1.0, 16.0)),
        "dt_bias": dt + jnp.log(-jnp.expm1(-dt)),
        "o_gain": 1.0 + nrm(ks[11], (DEPTH, HEAD_DV), 0.02),
        "w_pool": nrm(ks[12], (DEPTH, N_POOL_GROUPS, POOL_GROUP, POOL_GROUP), POOL_GROUP ** -0.5),
        "pool_scale": 1.0 + nrm(ks[13], (DEPTH, POOL_WIDTH), 0.02),
        "w_out": nrm(ks[14], (DEPTH, D_MODEL, D_MODEL), BETA_INIT * D_MODEL ** -0.5),
        "ln1_g": 1.0 + nrm(ks[15], (DEPTH, D_MODEL), 0.02),
        "ln1_b": nrm(ks[16], (DEPTH, D_MODEL), 0.02),
        "w_ff1": nrm(ks[17], (DEPTH, D_MODEL, D_FF), D_MODEL ** -0.5),
        "b_ff1": nrm(ks[18], (DEPTH, D_FF), 0.02),
        "w_ff2": nrm(ks[19], (DEPTH, D_FF, D_MODEL), BETA_INIT * D_FF ** -0.5),
        "b_ff2": nrm(ks[20], (DEPTH, D_MODEL), 0.02),
        "ln2_g": 1.0 + nrm(ks[21], (DEPTH, D_MODEL), 0.02),
        "ln2_b": nrm(ks[22], (DEPTH, D_MODEL), 0.02),
    }


def reference(x_prompt, x_sample, state_pool, state_conv, state_delta, ln_in_g, ln_in_b, w_in, conv_w, a_log,
              dt_bias, o_gain, w_pool, pool_scale, w_out, ln1_g, ln1_b, w_ff1, b_ff1, w_ff2, b_ff2, ln2_g, ln2_b):
    B = x_prompt.shape[0]
    dt = x_prompt.dtype
    xp = layer_norm(x_prompt, ln_in_g, ln_in_b)
    xs = layer_norm(x_sample, ln_in_g, ln_in_b)
    pool_p, conv_p, delta_p, pool_s, conv_s, delta_s = [], [], [], [], [], []
    for l in range(DEPTH):
        lw = (w_in[l], conv_w[l], a_log[l], dt_bias[l], o_gain[l], w_pool[l], pool_scale[l], w_out[l],
              ln1_g[l], ln1_b[l], w_ff1[l], b_ff1[l], w_ff2[l], b_ff2[l], ln2_g[l], ln2_b[l])
        xp, pp, cp, sp = trunk_layer(xp,
                                     jnp.zeros((B, POOL_STATE, POOL_WIDTH), dt),
                                     jnp.zeros((B, CONV_WIDTH - 1, CONV_CH), dt),
                                     jnp.zeros((B, N_HEADS, HEAD_DK, HEAD_DV), dt),
                                     0, *lw)
        xs, ps, cs, ss = trunk_layer(xs, state_pool[l], state_conv[l], state_delta[l], PAST_LEN, *lw)
        pool_p.append(pp); conv_p.append(cp); delta_p.append(sp)
        pool_s.append(ps); conv_s.append(cs); delta_s.append(ss)
    return (xp, xs, jnp.stack(pool_p), jnp.stack(conv_p), jnp.stack(delta_p),
            jnp.stack(pool_s), jnp.stack(conv_s), jnp.stack(delta_s))
```

```python
import numpy as np
from contextlib import ExitStack
import concourse.bass as bass
import concourse.mybir as mybir
from concourse.bass_utils import run_bass_kernel_spmd

F32 = mybir.dt.float32
BF16 = mybir.dt.bfloat16
AF = mybir.ActivationFunctionType
ALU = mybir.AluOpType
SEM_LIM = 8000
import os
STOP = int(os.environ.get("KSTOP", "9"))
SKIP_PROMPT = int(os.environ.get("KSKIP_PROMPT", "0"))
SKIP_SAMPLE = int(os.environ.get("KSKIP_SAMPLE", "0"))
KDL = int(os.environ.get("KDL", "0"))
NSET = int(os.environ.get("KNSET", "4"))
WSCR = int(os.environ.get("KWSCR", "1"))
BF_INV = int(os.environ.get("KBFINV", "0"))
NACC = int(os.environ.get("KNACC", "4"))
WQ2 = int(os.environ.get("KWQ2", "1"))
POOLOFF = int(os.environ.get("KPOOLOFF", "1"))
OVL = int(os.environ.get("KOVL", "0"))
SAMPLE_LAST = int(os.environ.get("KSLAST", "1"))
NFILL = int(os.environ.get("KNFILL", "0"))
FILLN = int(os.environ.get("KFILLN", "256"))
FP32R = int(os.environ.get("KFP32R", "0"))
PC = int(os.environ.get("KPC", "128"))
TT = int(os.environ.get("KTT", "512"))

D = 1024
NK = 8
DEPTH = 2
SEQ_FULL = 4096
IN_W = 7184
ALPHA = (2 * DEPTH) ** 0.25
LN_EPS = 1e-5
RMS_EPS = 1e-6
L2_EPS = 1e-6
LV = 180
V_LN1G, V_LN1B, V_LN2G, V_LN2B, V_BF2, V_PSC, V_BF1, V_CW, V_OG = 0, 8, 16, 24, 32, 40, 48, 80, 176
V_ING, V_INB = 2 * LV, 2 * LV + 8
NV = 2 * LV + 16
C_ID, C_TRI, C_NEGM, C_STR, C_ONE, C_CORR = 0, 128, 256, 384, 512, 640
NCST = 640 + 64


class Reg:
    __slots__ = ("name", "w", "rs")

    def __init__(self, name):
        self.name = name
        self.w = None
        self.rs = []


class Ins:
    __slots__ = ("eng", "fn", "deps", "ticket", "need", "dsem", "dticket", "idx")


class DSem:
    __slots__ = ("name", "count", "sem")

    def __init__(self, name):
        self.name = name
        self.count = 0
        self.sem = None


class Prog:
    ENGS = ("pe", "act", "dve", "pool", "sp")

    def __init__(self, nc):
        self.nc = nc
        self.ins = []
        self.out_dmas = []
        self.dsems = []

    def dsem(self, name):
        d = DSem(name)
        self.dsems.append(d)
        return d

    def add(self, eng, fn, reads=(), writes=(), dsem=None, is_out=False):
        i = Ins()
        i.eng = eng
        i.fn = fn
        i.idx = len(self.ins)
        i.need = False
        i.ticket = None
        i.dsem = dsem
        i.dticket = None
        deps = set()
        for r in reads:
            if r.w is not None:
                deps.add(r.w)
        for w in writes:
            if w.w is not None:
                deps.add(w.w)
            deps.update(w.rs)
        for r in reads:
            if dsem is None:
                r.rs = [x for x in r.rs if not (self.ins[x].eng == eng and self.ins[x].dsem is None)]
            r.rs.append(i.idx)
        for w in writes:
            w.w = i.idx
            w.rs = []
        deps.discard(i.idx)
        i.deps = sorted(deps)
        if dsem is not None:
            dsem.count += 1
            i.dticket = dsem.count
        for d in i.deps:
            self.ins[d].need = True
        self.ins.append(i)
        if is_out:
            self.out_dmas.append(i.idx)
        return i

    def fence(self, src_regs, dst_regs):
        pend = []
        for s in src_regs:
            if s.w is not None:
                pend.append(s.w)
            pend.extend(s.rs)
        for d in dst_regs:
            d.rs = list(set(d.rs) | set(pend))

    def emit(self, stack):
        nc = self.nc
        fin = Ins()
        fin.eng = "sp"
        fin.fn = None
        fin.idx = len(self.ins)
        fin.need = False
        fin.dsem = None
        fin.deps = list(self.out_dmas)
        fin.ticket = None
        fin.dticket = None
        self.ins.append(fin)
        cnt = {e: 0 for e in self.ENGS}
        for i in self.ins:
            if i.dsem is None and i.need:
                cnt[i.eng] += 1
                i.ticket = cnt[i.eng]
        esems = {}
        for e in self.ENGS:
            n = (cnt[e] + SEM_LIM - 1) // SEM_LIM
            esems[e] = [stack.enter_context(nc.semaphore(f"s_{e}{k}")) for k in range(max(n, 1))]
        for d in self.dsems:
            if d.count > 0:
                d.sem = stack.enter_context(nc.semaphore(f"d_{d.name}"))
        per_eng = {e: [i for i in self.ins if i.eng == e] for e in self.ENGS}
        ins_all = self.ins
        nwaits = [0]

        def run_engine(e, h):
            wm = {}
            maxk = {}
            for i in per_eng[e]:
                for d in i.deps:
                    di = ins_all[d]
                    if di.dsem is not None:
                        key = ("d", id(di.dsem))
                        val = di.dticket * 16
                        sem = di.dsem.sem
                    else:
                        if di.eng == e and e == "pe":
                            continue
                        k = (di.ticket - 1) // SEM_LIM
                        if maxk.get(di.eng, -1) > k:
                            continue
                        key = (di.eng, k)
                        val = (di.ticket - 1) % SEM_LIM + 1
                        sem = esems[di.eng][k]
                    if wm.get(key, 0) >= val:
                        continue
                    h.wait_ge(sem, val)
                    nwaits[0] += 1
                    wm[key] = val
                    if di.dsem is None:
                        maxk[di.eng] = max(maxk.get(di.eng, -1), key[1])
                if i.fn is None:
                    continue
                r = i.fn(h)
                if i.dsem is not None:
                    r.then_inc(i.dsem.sem, 16)
                elif i.need:
                    k = (i.ticket - 1) // SEM_LIM
                    r.then_inc(esems[e][k], 1)

        block = stack.enter_context(nc.Block())

        @block.tensor
        def _(h):
            run_engine("pe", h)

        @block.scalar
        def _(h):
            run_engine("act", h)

        @block.vector
        def _(h):
            run_engine("dve", h)

        @block.gpsimd
        def _(h):
            run_engine("pool", h)

        @block.sync
        def _(h):
            run_engine("sp", h)

        return dict(n_ins=len(self.ins), n_waits=nwaits[0], cnt=cnt)


class Buf:
    total = 0
    sizes = []

    def __init__(self, P, st, nc, name, shape, dtype, nreg=1):
        self.t = st.enter_context(nc.sbuf_tensor(name, shape, dtype))
        nb = int(np.prod(shape[1:])) * (2 if dtype == BF16 else 4)
        Buf.total += (nb + 31) // 32 * 32
        Buf.sizes.append((name, nb))
        self.r = [Reg(f"{name}{i}") for i in range(nreg)]
        self.ds = [None] * nreg
        self.P = P
        self.name = name

    @property
    def R(self):
        return self.r

    def dsem(self, i=0):
        if self.ds[i] is None:
            self.ds[i] = self.P.dsem(f"{self.name}{i}")
        return self.ds[i]


def build_program(SEQ, with_sample=True):
    nc = bass.Bass("TRN2", target_bir_lowering=False)
    NT = SEQ // TT
    dr = lambda n, s, k: nc.dram_tensor(n, s, F32, kind=k).ap()
    xp_d = dr("xp", [SEQ, D], "ExternalInput")
    xs_d = dr("xs", [64, D], "ExternalInput")
    stp_d = dr("st_pool", [DEPTH, 2, 15, D], "ExternalInput")
    stc_d = dr("st_conv", [DEPTH, 2, 3, 3072], "ExternalInput")
    std_d = dr("st_delta", [DEPTH, 2, 8, 128, 128], "ExternalInput")
    w_in_d = dr("w_in", [DEPTH, D, IN_W], "ExternalInput")
    w_pool_d = dr("w_pool", [DEPTH, 4, 256, 256], "ExternalInput")
    w_out_d = dr("w_out", [DEPTH, D, D], "ExternalInput")
    w_ff1_d = dr("w_ff1", [DEPTH, D, 4096], "ExternalInput")
    w_ff2_d = dr("w_ff2", [DEPTH, 4096, D], "ExternalInput")
    vecs_d = dr("vecs", [128, NV], "ExternalInput")
    bc_d = dr("bc", [128, 32], "ExternalInput")
    cst_d = dr("cst", [128, NCST], "ExternalInput")
    yp_d = dr("yp", [SEQ, D], "ExternalOutput")
    ys_d = dr("ys", [64, D], "ExternalOutput")
    pp_d = dr("pool_p", [DEPTH, 15, D], "ExternalOutput")
    cp_d = dr("conv_p", [DEPTH, 3, 3072], "ExternalOutput")
    dp_d = dr("delta_p", [DEPTH, 8, 128, 128], "ExternalOutput")
    psm_d = dr("pool_s", [DEPTH, 2, 15, D], "ExternalOutput")
    csm_d = dr("conv_s", [DEPTH, 2, 3, 3072], "ExternalOutput")
    dsm_d = dr("delta_s", [DEPTH, 2, 8, 128, 128], "ExternalOutput")

    P = Prog(nc)
    st = ExitStack()
    with st:
        mk = lambda name, shape, dt=F32, nreg=1: Buf(P, st, nc, name, shape, dt, nreg)
        CST = mk("CST", [128, NCST])
        VEC = mk("VEC", [128, NV])
        BC = mk("BC", [128, 32])
        NEGA = mk("NEGA", [128, 16])
        ONEB = mk("ONEB", [128, 128], BF16)
        IDENT = CST.t[:, C_ID:C_ID + 128]
        TRI_B = mk("TRIB", [128, 128])
        TRI = TRI_B.t[:, :]
        ONES_B = mk("ONESB", [128, 128])
        NEGM = CST.t[:, C_NEGM:C_NEGM + 128]
        STRICT = CST.t[:, C_STR:C_STR + 128]
        ONES = ONES_B.t[:, :]
        ONESN = mk("ONESN", [128, 128])
        X32 = mk("X32", [128, NK, TT], F32, NK)
        XB = mk("XB", [128, NK, TT], BF16, NK)
        XIN = [mk(f"XIN{i}", [128, D]) for i in range(2)]
        STAT = mk("STAT", [128, 16])
        UE = [mk(f"UE{i}", [128, 16 + TT]) for i in range(2)]
        SA = mk("SA", [128, 16 + TT])
        SB_ = mk("SB", [128, 16 + TT])
        PH = mk("PH", [128, DEPTH, NK, 2, 16], F32, DEPTH)
        YA = mk("YA", [128, NK, TT], BF16, NK)
        SCR = [mk(f"SCR{i}", [128, TT]) for i in range(NACC)]
        CE = [mk(f"CE{i}", [128, 4 + TT]) for i in range(2)]
        ACC = [mk(f"ACC{i}", [128, TT]) for i in range(NACC)]
        CH = mk("CH", [128, DEPTH, 24, 2, 4], F32, DEPTH)
        HB = mk("HB", [128, 32, TT], BF16, 32)
        QT_t = HB.t[:, 0:8, :]
        KT_t = HB.t[:, 8:16, :]
        QKV = mk("QKVR", [1, 4], F32, 16)
        KTM = mk("KTM", [128, TT // PC, D], BF16, 1)
        VTM = mk("VTM", [128, TT // PC, D], BF16, 1)
        GZ = mk("GZ", [128, NK, TT], BF16, NK)
        MIX = GZ
        OT = XB
        RS = mk("RS", [128, TT])
        RS2 = mk("RS2", [128, TT])
        S32 = mk("S32", [128, DEPTH, 8, 128], F32, DEPTH * 8)
        SBF = mk("SBF", [128, DEPTH, 8, 128], BF16, DEPTH * 8)
        BETA = mk("BETA", [128, TT // PC, 8])
        GG = mk("GG", [128, TT // PC, 8])
        BAT = mk("BAT", [128, 8])
        WBA = mk("WBA", [128, NK, 16], BF16)
        STG = mk("STG", [16, D])
        STG2 = mk("STG2", [4, 512])
        NSLOT = 3
        WS = [mk(f"W{i}", [128, NK, 512], BF16) for i in range(NSLOT)]
        dl = []
        for s in range(NSET):
            d_ = {}
            for n in ("GH", "EE"):
                d_[n] = mk(f"{n}{s}", [128, PC])
            IDT = BF16 if BF_INV else (mybir.dt.float32r if FP32R else F32)
            d_["PX"] = mk(f"PX{s}", [128, PC], IDT)
            for n in ("AAa", "AAb"):
                d_[n] = mk(f"{n}{s}", [128, 2, PC], IDT)
            for n in ("ATT", "KD", "QD") + (() if BF_INV else ("NTB",)):
                d_[n] = mk(f"{n}{s}", [128, PC], BF16)
            for n in ("RP", "VNB", "KW"):
                d_[n] = mk(f"{n}{s}", [128, 128], BF16)
            dl.append(d_)
        pc = []
        for s in range(2):
            d_ = {}
            for n in ("GCC", "NGCC", "GL", "WJ", "SDEC", "TW"):
                d_[n] = mk(f"{n}{s}", [128, 8])
            pc.append(d_)
        PS = [st.enter_context(nc.psum_tensor(f"ps{i}", [128, 512], F32)) for i in range(8)]
        PSR = [Reg(f"ps{i}") for i in range(8)]
        psi = [0]

        def nps():
            b = psi[0] % (7 if NFILL else 8)
            psi[0] += 1
            return b

        def dve_tt(out, a, b, op, R, W):
            P.add("dve", lambda h: h.tensor_tensor(out=out, in0=a, in1=b, op=op), R, W)

        def dve_ts(out, a, s1, s2, op0, op1, R, W):
            if op1 is None:
                P.add("dve", lambda h: h.tensor_scalar(out=out, in0=a, scalar1=s1, scalar2=None, op0=op0), R, W)
            else:
                P.add("dve", lambda h: h.tensor_scalar(out=out, in0=a, scalar1=s1, scalar2=s2, op0=op0, op1=op1), R, W)

        def dve_stt(out, a, s, b, op0, op1, R, W):
            P.add("dve", lambda h: h.scalar_tensor_tensor(out=out, in0=a, scalar=s, in1=b, op0=op0, op1=op1), R, W)

        def pool_cp(out, a, R, W):
            P.add("pool", lambda h: h.tensor_copy(out=out, in_=a), R, W)

        def dve_cp(out, a, R, W):
            P.add("dve", lambda h: h.tensor_copy(out=out, in_=a), R, W)

        def act(out, a, func, R, W, bias=None, scale=None):
            kw = {}
            if bias is not None:
                kw["bias"] = bias
            if scale is not None:
                kw["scale"] = scale
            P.add("act", lambda h: h.activation(out=out, in_=a, func=func, **kw), R, W)

        def mm(out, lhsT, rhs, start, stop, R, W):
            P.add("pe", lambda h: h.matmul(out, lhsT, rhs, start=start, stop=stop), R, W)

        def tr(out, in_, n, R, W):
            P.add("pe", lambda h: h.transpose(out, in_, IDENT[0:n, 0:n]), R + [CST.r[0]], W)

        def dma(q, out, in_, ds, R, W, is_out=False):
            P.add(q, lambda h: h.dma_start(out=out, in_=in_), R, W, dsem=ds, is_out=is_out)

        def vcol(l, off, k=0):
            c = l * LV + off + k
            return VEC.t[:, c:c + 1]

        def layer_blocks(l):
            bl = []
            kp = lambda ap: ap.rearrange("(k p) c -> p k c", p=128)
            for i in range(2):
                bl.append((f"up{i}", kp(w_in_d[l, :, i * 512:(i + 1) * 512]), 512))
            bl.append(("wp", w_pool_d[l].rearrange("g (c p) d -> p (g c) d", p=128), 256))
            for i in range(2):
                bl.append((f"ga{i}", kp(w_in_d[l, :, 5120 + i * 512:5120 + (i + 1) * 512]), 512))
            for i in range(6):
                bl.append((f"qkv{i}", kp(w_in_d[l, :, 1024 + i * 512:1024 + (i + 1) * 512]), 512))
            for i in range(2):
                bl.append((f"z{i}", kp(w_in_d[l, :, 4096 + i * 512:4096 + (i + 1) * 512]), 512))
            for i in range(2):
                bl.append((f"gb{i}", kp(w_in_d[l, :, 6144 + i * 512:6144 + (i + 1) * 512]), 512))
            for i in range(2):
                bl.append((f"wo{i}", kp(w_out_d[l, :, i * 512:(i + 1) * 512]), 512))
            for i in range(8):
                bl.append((f"f1_{i}", kp(w_ff1_d[l, :, i * 512:(i + 1) * 512]), 512))
            for c in range(2):
                for k in range(4):
                    bl.append((f"f2_{c}_{k}", kp(w_ff2_d[l, k * 1024:(k + 1) * 1024, c * 512:(c + 1) * 512]), 512))
            return bl

        npass = NT + (1 if with_sample else 0)
        allblocks = []
        for _ in range(npass):
            for l in range(DEPTH):
                allblocks.extend(layer_blocks(l))
        wstate = dict(issued=0, used=0)

        NB = 2 * len(layer_blocks(0))
        wscr = nc.dram_tensor("wscr", [NB, 128, NK * 512], BF16, kind="Internal").ap()
        use_scr = (STOP >= 9 and not SKIP_PROMPT and not SKIP_SAMPLE and WSCR)

        def w_issue():
            i = wstate["issued"]
            if i >= len(allblocks):
                return
            name, src, ncol = allblocks[i]
            s = WS[i % NSLOT]
            j = i % NB
            scr = wscr[j, :, 0:NK * ncol].rearrange("p (k c) -> p k c", k=NK)
            if not hasattr(s, "hwds"):
                s.hwds = P.dsem(f"{s.name}hw")
            if use_scr and i >= NB:
                if WQ2 and i % 2 == 0:
                    dma("sp", s.t[:, :, 0:ncol], scr, s.hwds, [], s.R)
                else:
                    dma("pool", s.t[:, :, 0:ncol], scr, s.dsem(), [], s.R)
            else:
                dma("pool", s.t[:, :, 0:ncol], src, s.dsem(), [], s.R)
                if use_scr:
                    dma("sp", scr, s.t[:, :, 0:ncol], s.hwds, s.R, [])
            wstate["issued"] += 1

        def w_next(name):
            i = wstate["used"]
            if STOP < 9 or SKIP_PROMPT or SKIP_SAMPLE:
                while allblocks[i][0] != name:
                    del allblocks[i]
            assert allblocks[i][0] == name, (allblocks[i][0], name)
            while wstate["issued"] < min(i + NSLOT, len(allblocks)):
                w_issue()
            wstate["used"] += 1
            return WS[i % NSLOT]

        dma("sp", CST.t[:], cst_d, CST.dsem(), [], CST.R)
        dma("sp", VEC.t[:], vecs_d, VEC.dsem(), [], VEC.R)
        dma("sp", BC.t[:], bc_d, BC.dsem(), [], BC.R)
        for l in range(DEPTH):
            act(NEGA.t[:, l * 8:(l + 1) * 8], BC.t[:, l * 16:l * 16 + 8], AF.Exp, BC.R, NEGA.R)
        dve_ts(NEGA.t[:], NEGA.t[:], -1.0, None, ALU.mult, None, NEGA.R, NEGA.R)
        dve_cp(TRI_B.t[:], CST.t[:, C_TRI:C_TRI + 128], CST.R, CST.R)
        dve_cp(ONES_B.t[:], CST.t[:, C_ONE:C_ONE + 128], CST.R, CST.R)
        dve_cp(ONEB.t[:], ONES, CST.R, ONEB.R)
        dve_ts(ONESN.t[:], ONES, 1.0 / D, None, ALU.mult, None, CST.R, ONESN.R)

        for bf in UE + CE + [SA, SB_]:
            P.add("dve", lambda h, bf=bf: h.memset(bf.t[:], 0.0), [], bf.R)
        class Ctx:
            pass

        def ln_fm(ctx, l, goff, boff):
            T = ctx.T
            b = nps()
            for k in range(NK):
                mm(PS[b][:, 0:T], ONESN.t[:], X32.t[:, k, 0:T], k == 0, k == NK - 1, [ONESN.r[0], X32.r[k]], [PSR[b]])
            for k in range(NK):
                dve_tt(X32.t[:, k, 0:T], X32.t[:, k, 0:T], PS[b][:, 0:T], ALU.subtract, [X32.r[k], PSR[b]], [X32.r[k]])
            b2 = nps()
            for k in range(NK):
                s = SCR[k % 2]
                act(s.t[:, 0:T], X32.t[:, k, 0:T], AF.Square, [X32.r[k]], s.R)
                mm(PS[b2][:, 0:T], ONESN.t[:], s.t[:, 0:T], k == 0, k == NK - 1, [ONESN.r[0], s.r[0]], [PSR[b2]])
            act(RS2.t[:, 0:T], PS[b2][:, 0:T], AF.Ln, [PSR[b2], EPSB.r[0]], RS2.R, bias=EPS_LN)
            act(RS.t[:, 0:T], RS2.t[:, 0:T], AF.Exp, RS2.R, RS.R, scale=-0.5)
            for k in range(NK):
                dve_stt(X32.t[:, k, 0:T], X32.t[:, k, 0:T], vcol(l, goff, k) if l >= 0 else None, RS.t[:, 0:T],
                        ALU.mult, ALU.mult, [X32.r[k], RS.r[0], VEC.r[0]], [X32.r[k]])
                act(X32.t[:, k, 0:T], X32.t[:, k, 0:T], AF.Identity, [X32.r[k], VEC.r[0]], [X32.r[k]],
                    bias=vcol(l, boff, k))
                if POOLOFF:
                    pool_cp(XB.t[:, k, 0:T], X32.t[:, k, 0:T], [X32.r[k]], [XB.r[k]])
                else:
                    act(XB.t[:, k, 0:T], X32.t[:, k, 0:T], AF.Copy, [X32.r[k]], [XB.r[k]])

        def proj(ctx, slot, nchunk, rhs, rhs_regs, evac, coff=0):
            T = ctx.T
            for m in range(nchunk):
                b = nps()
                for k in range(NK):
                    mm(PS[b][:, 0:T], slot.t[:, k, coff + m * 128:coff + (m + 1) * 128], rhs[:, k, 0:T], k == 0, k == NK - 1,
                       [slot.r[0], rhs_regs[k]], [PSR[b]])
                evac(m, b)

        EPSB = mk("EPSB", [128, 4])
        P.add("dve", lambda h: h.memset(EPSB.t[:, 0:1], LN_EPS), [], EPSB.R)
        P.add("dve", lambda h: h.memset(EPSB.t[:, 1:2], RMS_EPS), [], EPSB.R)
        P.add("dve", lambda h: h.memset(EPSB.t[:, 2:3], L2_EPS), [], EPSB.R)
        P.add("dve", lambda h: h.memset(EPSB.t[:, 3:4], 1.0), [], EPSB.R)
        EPS_LN = EPSB.t[:, 0:1]
        EPS_RMS = EPSB.t[:, 1:2]
        EPS_L2 = EPSB.t[:, 2:3]
        ONE_B = EPSB.t[:, 3:4]

        def load_input(ctx):
            T = ctx.T
            for r in range(ctx.NRG):
                n = ctx.RGN
                xi = XIN[r % 2]
                dma("sp", xi.t[0:n, :], ctx.x_d[ctx.tok0 + r * n: ctx.tok0 + (r + 1) * n, :], xi.dsem(), [], xi.R)
                P.add("dve", lambda h, xi=xi, n=n: h.bn_stats(out=STAT.t[0:n, 0:6], in_=xi.t[0:n, 0:512]), xi.R, STAT.R)
                P.add("dve", lambda h, xi=xi, n=n: h.bn_stats(out=STAT.t[0:n, 6:12], in_=xi.t[0:n, 512:1024]), xi.R, STAT.R)
                P.add("dve", lambda h, n=n: h.bn_aggr(out=STAT.t[0:n, 12:14], in_=STAT.t[0:n, 0:12]), STAT.R, STAT.R)
                act(STAT.t[0:n, 14:15], STAT.t[0:n, 13:14], AF.Ln, STAT.R + EPSB.R, STAT.R, bias=EPS_LN[0:n])
                act(STAT.t[0:n, 15:16], STAT.t[0:n, 14:15], AF.Exp, STAT.R, STAT.R, scale=-0.5)
                dve_ts(xi.t[0:n, :], xi.t[0:n, :], STAT.t[0:n, 12:13], STAT.t[0:n, 15:16], ALU.subtract, ALU.mult,
                       xi.R + STAT.R, xi.R)
                for half in range(2):
                    b = nps()
                    for kk in range(4):
                        k = half * 4 + kk
                        tr(PS[b][:, kk * 128:kk * 128 + n], xi.t[0:n, k * 128:(k + 1) * 128], n, xi.R, [PSR[b]])
                    for kk in range(4):
                        k = half * 4 + kk
                        act(X32.t[:, k, r * n:(r + 1) * n], PS[b][:, kk * 128:kk * 128 + n], AF.Identity,
                            [PSR[b], VEC.r[0]], [X32.r[k]], bias=VEC.t[:, V_INB + k:V_INB + k + 1],
                            scale=VEC.t[:, V_ING + k:V_ING + k + 1])
                        dve_cp(XB.t[:, k, r * n:(r + 1) * n], X32.t[:, k, r * n:(r + 1) * n], [X32.r[k]], [XB.r[k]])

        def store_output(ctx):
            T = ctx.T
            for r in range(ctx.NRG):
                n = ctx.RGN
                xi = XIN[r % 2]
                for half in range(2):
                    b = nps()
                    for kk in range(4):
                        k = half * 4 + kk
                        tr(PS[b][0:n, kk * 128:(kk + 1) * 128], X32.t[:, k, r * n:(r + 1) * n], 128, [X32.r[k]], [PSR[b]])
                    act(xi.t[0:n, half * 512:(half + 1) * 512], PS[b][0:n, :], AF.Copy, [PSR[b]], xi.R)
                dma("sp", ctx.y_d[ctx.tok0 + r * n: ctx.tok0 + (r + 1) * n, :], xi.t[0:n, :], xi.dsem(), xi.R, [], is_out=True)

        def fm_state_load(ctx, l):
            for s in range(ctx.NS):
                dma("sp", STG.t[0:15, :], stp_d[l, s], STG.dsem(), [], STG.R)
                b = nps()
                for k in range(NK):
                    tr(PS[b][:, k * 16:k * 16 + 15], STG.t[0:15, k * 128:(k + 1) * 128], 15, STG.R, [PSR[b]])
                dve_cp(PH.t[:, l, :, s, 1:16], PS[b][:, 0:128].rearrange("p (k c) -> p k c", c=16)[:, :, 0:15], [PSR[b]], [PH.r[l]])
                for g in range(6):
                    dma("sp", STG2.t[0:3, :], stc_d[l, s, :, g * 512:(g + 1) * 512], STG2.dsem(), [], STG2.R)
                    b = nps()
                    for m in range(4):
                        tr(PS[b][:, m * 4:m * 4 + 3], STG2.t[0:3, m * 128:(m + 1) * 128], 3, STG2.R, [PSR[b]])
                    dve_cp(CH.t[:, l, g * 4:(g + 1) * 4, s, 1:4], PS[b][:, 0:16].rearrange("p (k c) -> p k c", c=4)[:, :, 0:3],
                           [PSR[b]], [CH.r[l]])

        def state_store(ctx, l):
            for s in range(ctx.NS):
                for half in range(2):
                    b = nps()
                    for kk in range(4):
                        k = half * 4 + kk
                        tr(PS[b][0:15, kk * 128:(kk + 1) * 128], PH.t[:, l, k, s, 1:16], 128, [PH.r[l]], [PSR[b]])
                    act(STG.t[0:15, half * 512:(half + 1) * 512], PS[b][0:15, :], AF.Copy, [PSR[b]], STG.R)
                dma("sp", ctx.pool_out(l, s), STG.t[0:15, :], STG.dsem(), STG.R, [], is_out=True)
                for g in range(6):
                    b = nps()
                    for mm_ in range(4):
                        m = g * 4 + mm_
                        tr(PS[b][0:3, mm_ * 128:(mm_ + 1) * 128], CH.t[:, l, m, s, 1:4], 128, [CH.r[l]], [PSR[b]])
                    act(STG2.t[0:3, :], PS[b][0:3, :], AF.Copy, [PSR[b]], STG2.R)
                    dma("sp", ctx.conv_out(l, s)[:, g * 512:(g + 1) * 512], STG2.t[0:3, :], STG2.dsem(), STG2.R, [], is_out=True)

        def to_tm(dst, hh, src_t, src_R, ctx, dst_regs):
            C, NCH = ctx.C, ctx.NCH
            for g0 in range(0, NCH, 4):
                n = min(4, NCH - g0)
                bb = nps()
                for j in range(n):
                    ci = g0 + j
                    tr(PS[bb][0:C, j * 128:(j + 1) * 128], src_t[:, ci * C:(ci + 1) * C], 128, src_R, [PSR[bb]])
                act(dst.t[0:C, g0:g0 + n, hh * 128:(hh + 1) * 128], PS[bb][0:C, 0:n * 128].rearrange("p (a b) -> p a b", b=128),
                    AF.Copy, [PSR[bb]], dst_regs)

        def delta_phase(ctx, l):
            T, NS, L, C, NCH = ctx.T, ctx.NS, ctx.L, ctx.C, ctx.NCH
            K = {128: 6, 64: 5, 32: 4}[C]
            Sregs = [S32.r[l * 8 + h] for h in range(8)]
            SBregs = [SBF.r[l * 8 + h] for h in range(8)]

            pcnt = [0, 0]

            def nps_c():
                if not OVL:
                    return nps()
                pcnt[0] += 1
                return (pcnt[0] - 1) % 4

            def nps_p():
                if not OVL:
                    return nps()
                pcnt[1] += 1
                return 4 + (pcnt[1] - 1) % 4

            def perchunk(ci):
                pcs = pc[ci % 2]
                b = nps()
                mm(PS[b][0:C, 0:8], TRI[0:C, 0:C], GG.t[0:C, ci, :], True, True, CST.R + GG.R, [PSR[b]])
                mm(PS[b][:, 8:16], ONES[0:C, :], GG.t[0:C, ci, :], True, True, CST.R + GG.R, [PSR[b]])
                dve_cp(pcs["GCC"].t[0:C, :], PS[b][0:C, 0:8], [PSR[b]], pcs["GCC"].R)
                dve_ts(pcs["NGCC"].t[0:C, :], PS[b][0:C, 0:8], -1.0, None, ALU.mult, None, [PSR[b]], pcs["NGCC"].R)
                dve_cp(pcs["GL"].t[:, :], PS[b][:, 8:16], [PSR[b]], pcs["GL"].R)
                dve_tt(pcs["TW"].t[0:C, :], pcs["GL"].t[0:C, :], pcs["GCC"].t[0:C, :], ALU.subtract,
                       pcs["GL"].R + pcs["GCC"].R, pcs["TW"].R)
                act(pcs["WJ"].t[0:C, :], pcs["TW"].t[0:C, :], AF.Exp, pcs["TW"].R, pcs["WJ"].R)
                act(pcs["SDEC"].t[:, :], pcs["GL"].t[:, :], AF.Exp, pcs["GL"].R, pcs["SDEC"].R)


            def pre(ci, h):
                c0 = ci * C
                pcs = pc[ci % 2]
                d = dl[h % NSET]
                G1 = d["GH"]
                qs = QT_t[:, h, c0:c0 + C]
                ks = KT_t[:, h, c0:c0 + C]
                qr, kr = QKV.r[h], QKV.r[8 + h]
                dve_ts(G1.t[0:C, 0:C], TRI[0:C, 0:C], GG.t[0:C, ci, h:h + 1], None, ALU.mult, None, CST.R + GG.R, G1.R)
                yield
                b1 = nps_p()
                mm(PS[b1][:, 0:C], ONES[0:C, :], G1.t[0:C, 0:C], True, True, CST.R + G1.R, [PSR[b1]])
                yield
                dve_tt(G1.t[0:C, 0:C], PS[b1][0:C, 0:C], NEGM[0:C, 0:C], ALU.add, [PSR[b1]] + CST.R, G1.R)
                yield
                act(d["EE"].t[:, 0:C], PS[b1][:, 0:C], AF.Exp, [PSR[b1]] + G1.R, d["EE"].R)
                act(G1.t[0:C, 0:C], G1.t[0:C, 0:C], AF.Exp, G1.R + pcs["NGCC"].R, G1.R, bias=pcs["NGCC"].t[0:C, h:h + 1])
                b2 = nps_p()
                mm(PS[b2][0:C, 0:C], ks, qs, True, True, [kr, qr], [PSR[b2]])
                mm(PS[b2][0:C, C:2 * C], ks, ks, True, True, [kr], [PSR[b2]])
                yield
                dve_tt(d["ATT"].t[0:C, 0:C], PS[b2][0:C, 0:C], G1.t[0:C, 0:C], ALU.mult, [PSR[b2]] + G1.R, d["ATT"].R)
                dve_stt(G1.t[0:C, 0:C], G1.t[0:C, 0:C], BETA.t[0:C, ci, h:h + 1], STRICT[0:C, 0:C], ALU.mult, ALU.mult,
                        G1.R + BETA.R + CST.R, G1.R)
                A, Bn = d["AAa"], d["AAb"]
                dve_tt(G1.t[0:C, 0:C], PS[b2][0:C, C:2 * C], G1.t[0:C, 0:C], ALU.mult, [PSR[b2]] + G1.R, G1.R)
                yield
                b3 = nps_p()
                tr(PS[b3][0:C, 0:C], G1.t[0:C, 0:C], C, G1.R, [PSR[b3]])
                dve_tt(d["PX"].t[0:C, 0:C], IDENT[0:C, 0:C], G1.t[0:C, 0:C], ALU.subtract, CST.R + G1.R, d["PX"].R)
                act(A.t[0:C, 0, 0:C], G1.t[0:C, 0:C], AF.Copy, G1.R, A.R)
                yield
                act(A.t[0:C, 1, 0:C], PS[b3][0:C, 0:C], AF.Copy, [PSR[b3]], A.R)
                yield
                cur, nxt = A, Bn
                for k in range(1, K + 1):
                    b4 = nps_p()
                    rr = (lambda ap: ap) if (not FP32R or C == 128) else (lambda ap: ap.bitcast(F32))
                    if k < K:
                        mm(PS[b4][0:C, 0:C], rr(cur.t[0:C, 1, 0:C]), rr(cur.t[0:C, 0, 0:C]), True, True, cur.R, [PSR[b4]])
                    mm(PS[b4][0:C, C:2 * C], rr(cur.t[0:C, 0, 0:C]), rr(cur.t[0:C, 1, 0:C]), True, True, cur.R, [PSR[b4]])
                    yield
                    if k < K:
                        act(nxt.t[0:C, :, 0:C], PS[b4][0:C, 0:2 * C].rearrange("p (a b) -> p a b", a=2), AF.Copy, [PSR[b4]], nxt.R)
                    else:
                        act(nxt.t[0:C, 1, 0:C], PS[b4][0:C, C:2 * C], AF.Copy, [PSR[b4]], nxt.R)
                    yield
                    b5 = nps_p()
                    mm(PS[b5][0:C, 0:C], rr(nxt.t[0:C, 1, 0:C]), rr(d["PX"].t[0:C, 0:C]), True, True, nxt.R + d["PX"].R, [PSR[b5]])
                    yield
                    dve_tt(d["PX"].t[0:C, 0:C], d["PX"].t[0:C, 0:C], PS[b5][0:C, 0:C], ALU.add, d["PX"].R + [PSR[b5]], d["PX"].R)
                    yield
                    cur, nxt = nxt, cur
                if not BF_INV:
                    act(d["NTB"].t[0:C, 0:C], d["PX"].t[0:C, 0:C], AF.Copy, d["PX"].R, d["NTB"].R)
                dve_tt(d["KD"].t[:, 0:C], ks, d["EE"].t[:, 0:C], ALU.mult, [kr] + d["EE"].R, d["KD"].R)
                dve_tt(d["QD"].t[:, 0:C], qs, d["EE"].t[:, 0:C], ALU.mult, [qr] + d["EE"].R, d["QD"].R)
                dve_ts(d["KW"].t[0:C, :], KTM.t[0:C, ci, h * 128:(h + 1) * 128], pcs["WJ"].t[0:C, h:h + 1], None, ALU.mult, None,
                       KTM.R + pcs["WJ"].R, d["KW"].R)
                yield


            def chain(ci, h):
                c0 = ci * C
                pcs = pc[ci % 2]
                d = dl[h % NSET]
                Sr, SBr = Sregs[h], SBregs[h]
                b1 = nps_c()
                mm(PS[b1][0:C, 0:128], d["KD"].t[:, 0:C], SBF.t[:, l, h, :], True, True, d["KD"].R + [SBr], [PSR[b1]])
                yield
                dve_tt(d["RP"].t[0:C, :], VTM.t[0:C, ci, h * 128:(h + 1) * 128], PS[b1][0:C, 0:128], ALU.subtract,
                       VTM.R + [PSR[b1]], d["RP"].R)
                yield
                b2 = nps_c()
                NTt = d["PX"] if BF_INV else d["NTB"]
                mm(PS[b2][0:C, 0:128], NTt.t[0:C, 0:C], d["RP"].t[0:C, :], True, True, NTt.R + d["RP"].R, [PSR[b2]])
                yield
                act(d["VNB"].t[0:C, :], PS[b2][0:C, 0:128], AF.Copy, [PSR[b2]] + BETA.R, d["VNB"].R, scale=BETA.t[0:C, ci, h:h + 1])
                yield
                b3 = nps_c()
                mm(PS[b3][:, 0:C], SBF.t[:, l, h, :], d["QD"].t[:, 0:C], True, False, [SBr] + d["QD"].R, [PSR[b3]])
                mm(PS[b3][:, 0:C], d["VNB"].t[0:C, :], d["ATT"].t[0:C, 0:C], False, True, d["VNB"].R + d["ATT"].R, [PSR[b3]])
                yield
                act(OT.t[:, h, c0:c0 + C], PS[b3][:, 0:C], AF.Copy, [PSR[b3]], [OT.r[h]])
                b4 = nps_c()
                mm(PS[b4][:, 0:128], d["KW"].t[0:C, :], d["VNB"].t[0:C, :], True, True, d["KW"].R + d["VNB"].R, [PSR[b4]])
                yield
                dve_stt(S32.t[:, l, h, :], S32.t[:, l, h, :], pcs["SDEC"].t[:, h:h + 1], PS[b4][:, 0:128], ALU.mult, ALU.add,
                        [Sr, PSR[b4]] + pcs["SDEC"].R, [Sr])
                yield
                act(SBF.t[:, l, h, :], S32.t[:, l, h, :], AF.Copy, [Sr], [SBr])
                yield


            def lockstep(gens):
                live = list(gens)
                while live:
                    nxt_live = []
                    for g in live:
                        try:
                            next(g)
                            nxt_live.append(g)
                        except StopIteration:
                            pass
                    live = nxt_live
                    for _ in range(NFILL):
                        mm(PS[7][:, 0:FILLN], ONEB.t[:], HB.t[:, 31, 0:FILLN], True, True, [ONEB.r[0]], [PSR[7]])


            groups = [(ci, h0) for ci in range(NCH) for h0 in range(0, 8, NSET)]
            if not OVL:
                for ci in range(NCH):
                    if ctx.sample:
                        dma("sp", S32.t[:, l, :, :], std_d[l, ci].rearrange("h k v -> k h v"), S32.dsem(l), [], Sregs)
                        for h in range(8):
                            act(SBF.t[:, l, h, :], S32.t[:, l, h, :], AF.Copy, [Sregs[h]], [SBregs[h]])
                    perchunk(ci)
                    for h0 in range(0, 8, NSET):
                        lockstep([pre(ci, h) for h in range(h0, h0 + NSET)])
                        lockstep([chain(ci, h) for h in range(h0, h0 + NSET)])
                    if ctx.sample or (ctx.last and ci == NCH - 1):
                        dma("sp", ctx.delta_out(l, ci).rearrange("h k v -> k h v"), S32.t[:, l, :, :], S32.dsem(l), Sregs, [], is_out=True)
                return
            perchunk(0)
            lockstep([pre(0, h) for h in range(0, NSET)])
            for gi, (ci, h0) in enumerate(groups):
                gens = [chain(ci, h) for h in range(h0, h0 + NSET)]
                if gi + 1 < len(groups):
                    cj, hj = groups[gi + 1]
                    if hj == 0:
                        perchunk(cj)
                    gens += [pre(cj, h) for h in range(hj, hj + NSET)]
                if h0 == 0 and ctx.sample:
                    dma("sp", S32.t[:, l, :, :], std_d[l, ci].rearrange("h k v -> k h v"), S32.dsem(l), [], Sregs)
                    for h in range(8):
                        act(SBF.t[:, l, h, :], S32.t[:, l, h, :], AF.Copy, [Sregs[h]], [SBregs[h]])
                lockstep(gens)
                if h0 + NSET >= 8 and (ctx.sample or (ctx.last and ci == NCH - 1)):
                    dma("sp", ctx.delta_out(l, ci).rearrange("h k v -> k h v"), S32.t[:, l, :, :], S32.dsem(l), Sregs, [], is_out=True)

        def layer(ctx, l):
            T, NS, L, C, NCH = ctx.T, ctx.NS, ctx.L, ctx.C, ctx.NCH
            EXT = NS * (16 + L)
            dma("pool", WBA.t[:], w_in_d[l, :, 7168:7184].rearrange("(k p) c -> p k c", p=128), WBA.dsem(), [], WBA.R)
            for blk in range(2):
                slot = w_next(f"up{blk}")

                def ev_pool(m, b, blk=blk):
                    kc = blk * 4 + m
                    g = kc // 2
                    w = 2 << g
                    ue = UE[kc % 2]
                    uv = ue.t[:, 0:EXT].rearrange("p (s e) -> p s e", s=NS)
                    dve_cp(uv[:, :, 1:16], PH.t[:, l, kc, 0:NS, 1:16], [PH.r[l]], ue.R)
                    act(uv[:, :, 16:16 + L], PS[b][:, 0:T].rearrange("p (s e) -> p s e", s=NS), AF.Copy, [PSR[b]], ue.R)
                    dve_cp(PH.t[:, l, kc, 0:NS, 1:16], uv[:, :, L + 1:L + 16], ue.R, [PH.r[l]])
                    src, srcR = ue.t, ue.R
                    sh = 1
                    pp = [SA, SB_]
                    i = 0
                    while sh < w:
                        dst = pp[i % 2]
                        dve_tt(dst.t[:, sh:EXT], src[:, sh:EXT], src[:, 0:EXT - sh], ALU.add, srcR, dst.R)
                        src, srcR = dst.t, dst.R
                        sh *= 2
                        i += 1
                    sv = src[:, 0:EXT].rearrange("p (s e) -> p s e", s=NS)
                    if ctx.first:
                        dve_tt(sv[:, 0, 16:32], sv[:, 0, 16:32], CST.t[:, C_CORR + g * 16:C_CORR + (g + 1) * 16], ALU.mult,
                               srcR + CST.R, srcR)
                    dve_stt(MIX.t[:, kc, 0:T].rearrange("p (s e) -> p s e", s=NS), sv[:, :, 16:16 + L], 1.0 / w,
                            uv[:, :, 16:16 + L], ALU.mult, ALU.subtract, srcR + ue.R, [MIX.r[kc]])

                proj(ctx, slot, 4, XB.t, XB.r, ev_pool)
            if STOP <= 1:
                return
            slot = w_next("wp")
            for dch in range(NK):
                g = dch // 2
                b = nps()
                for c in range(2):
                    mm(PS[b][:, 0:T], slot.t[:, g * 2 + c, (dch % 2) * 128:(dch % 2 + 1) * 128], MIX.t[:, g * 2 + c, 0:T],
                       c == 0, c == 1, [slot.r[0], MIX.r[g * 2 + c]], [PSR[b]])
                act(YA.t[:, dch, 0:T], PS[b][:, 0:T], AF.Copy, [PSR[b], VEC.r[0]], [YA.r[dch]], scale=vcol(l, V_PSC, dch))
            for blk in range(2):
                slot = w_next(f"ga{blk}")

                def ev_ga(m, b, blk=blk):
                    kc = blk * 4 + m
                    s = SCR[kc % 2]
                    act(s.t[:, 0:T], PS[b][:, 0:T], AF.Sigmoid, [PSR[b]], s.R)
                    dve_tt(YA.t[:, kc, 0:T], YA.t[:, kc, 0:T], s.t[:, 0:T], ALU.mult, [YA.r[kc]] + s.R, [YA.r[kc]])

                proj(ctx, slot, 4, XB.t, XB.r, ev_ga)
            if STOP <= 2:
                return
            P.fence(HB.R, QKV.R)
            pend = []
            for blk in range(6):
                slot = w_next(f"qkv{blk}")

                def ev_qkv(m, b, blk=blk):
                    mc = blk * 4 + m
                    ce = CE[mc % 2]
                    acc = ACC[mc % NACC]
                    cv = ce.t[:, 0:NS * (4 + L)].rearrange("p (s e) -> p s e", s=NS)
                    dve_cp(cv[:, :, 1:4], CH.t[:, l, mc, 0:NS, 1:4], [CH.r[l]], ce.R)
                    act(cv[:, :, 4:4 + L], PS[b][:, 0:T].rearrange("p (s e) -> p s e", s=NS), AF.Copy, [PSR[b]], ce.R)
                    dve_cp(CH.t[:, l, mc, 0:NS, 1:4], cv[:, :, L + 1:L + 4], ce.R, [CH.r[l]])
                    av = acc.t[:, 0:T].rearrange("p (s e) -> p s e", s=NS)
                    cw = lambda j: vcol(l, V_CW, j * 24 + mc)
                    dve_ts(av, cv[:, :, 1:1 + L], cw(0), None, ALU.mult, None, ce.R + VEC.R, acc.R)
                    for j in range(1, 4):
                        dve_stt(av, cv[:, :, 1 + j:1 + j + L], cw(j), av, ALU.mult, ALU.add, ce.R + acc.R + VEC.R, acc.R)
                    kind = mc // 8
                    hh = mc % 8
                    act(acc.t[:, 0:T], acc.t[:, 0:T], AF.Silu, acc.R, acc.R)
                    if kind == 2:
                        pend.append(lambda: to_tm(VTM, hh, acc.t, acc.R, ctx, VTM.R))
                        return
                    s = SCR[mc % NACC]
                    sq = s.t[:, 0:TT // 2].bitcast(BF16)[:, 0:T]
                    if POOLOFF:
                        P.add("pool", lambda h: h.tensor_tensor(out=sq, in0=acc.t[:, 0:T], in1=acc.t[:, 0:T], op=ALU.mult), acc.R, s.R)
                    else:
                        act(sq, acc.t[:, 0:T], AF.Square, acc.R, s.R)
                    pend.append(lambda: qk_post(kind, hh, acc, s, sq))

                def qk_post(kind, hh, acc, s, sq):
                    bb = nps()
                    mm(PS[bb][:, 0:T], ONEB.t[:], sq, True, True, [ONEB.r[0], s.r[0]], [PSR[bb]])
                    act(RS2.t[:, 0:T], PS[bb][:, 0:T], AF.Ln, [PSR[bb], EPSB.r[0]], RS2.R, bias=EPS_L2)
                    act(RS.t[:, 0:T], RS2.t[:, 0:T], AF.Exp, RS2.R, RS.R, scale=-0.5)
                    if kind == 0:
                        dve_stt(QT_t[:, hh, 0:T], acc.t[:, 0:T], 128.0 ** -0.5, RS.t[:, 0:T], ALU.mult, ALU.mult,
                                acc.R + RS.R, [QKV.r[hh]])
                    else:
                        dve_tt(acc.t[:, 0:T], acc.t[:, 0:T], RS.t[:, 0:T], ALU.mult, acc.R + RS.R, acc.R)
                        if POOLOFF:
                            pool_cp(KT_t[:, hh, 0:T], acc.t[:, 0:T], acc.R, [QKV.r[8 + hh]])
                        else:
                            act(KT_t[:, hh, 0:T], acc.t[:, 0:T], AF.Copy, acc.R, [QKV.r[8 + hh]])
                        to_tm(KTM, hh, acc.t, acc.R, ctx, KTM.R)

                def ev_qkv_d(m, b, blk=blk):
                    ev_qkv(m, b, blk)
                    if len(pend) >= NACC - 1:
                        pend.pop(0)()
                        pend.pop(0)()

                proj(ctx, slot, 4, XB.t, XB.r, ev_qkv_d)
            while pend:
                pend.pop(0)()
            if STOP <= 3:
                return
            for blk in range(2):
                slot = w_next(f"z{blk}")

                def ev_z(m, b, blk=blk):
                    kc = blk * 4 + m
                    act(GZ.t[:, kc, 0:T], PS[b][:, 0:T], AF.Silu, [PSR[b]], [GZ.r[kc]])

                proj(ctx, slot, 4, XB.t, XB.r, ev_z)
            for blk in range(2):
                slot = w_next(f"gb{blk}")

                def ev_gb(m, b, blk=blk):
                    kc = blk * 4 + m
                    s = SCR[kc % 2]
                    act(s.t[:, 0:T], PS[b][:, 0:T], AF.Sigmoid, [PSR[b]], s.R)
                    dve_tt(GZ.t[:, kc, 0:T], GZ.t[:, kc, 0:T], s.t[:, 0:T], ALU.mult, [GZ.r[kc]] + s.R, [GZ.r[kc]])

                proj(ctx, slot, 4, XB.t, XB.r, ev_gb)
            for ci in range(NCH):
                b = nps()
                for k in range(NK):
                    mm(PS[b][0:C, 0:16], XB.t[:, k, ci * C:(ci + 1) * C], WBA.t[:, k, :], k == 0, k == NK - 1,
                       [XB.r[k], WBA.r[0]], [PSR[b]])
                dve_tt(BAT.t[0:C, :], PS[b][0:C, 8:16], BC.t[0:C, l * 16 + 8:l * 16 + 16], ALU.add, [PSR[b]] + BC.R, BAT.R)
                act(BETA.t[0:C, ci, :], PS[b][0:C, 0:8], AF.Sigmoid, [PSR[b]] + BAT.R, BETA.R)
                act(BAT.t[0:C, :], BAT.t[0:C, :], AF.Exp, BAT.R, BAT.R)
                act(BAT.t[0:C, :], BAT.t[0:C, :], AF.Ln, BAT.R + EPSB.R, BAT.R, bias=ONE_B[0:C])
                dve_tt(GG.t[0:C, ci, :], BAT.t[0:C, :], NEGA.t[0:C, l * 8:(l + 1) * 8], ALU.mult, BAT.R + NEGA.R, GG.R)
            if STOP <= 4:
                return
            delta_phase(ctx, l)
            if STOP <= 5:
                return
            for h0 in range(0, NK, 4):
                bbs = {}
                for h in range(h0, h0 + 4):
                    s = SCR[h % NACC]
                    sq = s.t[:, 0:TT // 2].bitcast(BF16)[:, 0:T]
                    act(sq, OT.t[:, h, 0:T], AF.Square, [OT.r[h]], s.R)
                    bb = nps()
                    bbs[h] = bb
                    mm(PS[bb][:, 0:T], ONEB.t[:], sq, True, True, [ONEB.r[0], s.r[0]], [PSR[bb]])
                for h in range(h0, h0 + 4):
                    bb = bbs[h]
                    rs = RS if h % 2 == 0 else RS2
                    act(rs.t[:, 0:T], PS[bb][:, 0:T], AF.Ln, [PSR[bb], EPSB.r[0]], rs.R, bias=EPS_RMS, scale=1.0 / 128)
                    act(rs.t[:, 0:T], rs.t[:, 0:T], AF.Exp, rs.R, rs.R, scale=-0.5)
                    a = ACC[h % NACC]
                    dve_stt(a.t[:, 0:T], OT.t[:, h, 0:T], vcol(l, V_OG), rs.t[:, 0:T], ALU.mult, ALU.mult, [OT.r[h]] + rs.R + VEC.R, a.R)
                    dve_tt(a.t[:, 0:T], a.t[:, 0:T], GZ.t[:, h, 0:T], ALU.mult, a.R + [GZ.r[h]], a.R)
                    dve_tt(GZ.t[:, h, 0:T], a.t[:, 0:T], YA.t[:, h, 0:T], ALU.add, a.R + [YA.r[h]], [GZ.r[h]])
            for blk in range(2):
                slot = w_next(f"wo{blk}")

                def ev_wo(m, b, blk=blk):
                    kc = blk * 4 + m
                    dve_stt(X32.t[:, kc, 0:T], X32.t[:, kc, 0:T], ALPHA, PS[b][:, 0:T], ALU.mult, ALU.add, [X32.r[kc], PSR[b]], [X32.r[kc]])

                proj(ctx, slot, 4, GZ.t, GZ.r, ev_wo)
            ln_fm(ctx, l, V_LN1G, V_LN1B)
            if STOP <= 6:
                return
            P.fence(QKV.R, HB.R)
            for blk in range(8):
                slot = w_next(f"f1_{blk}")

                def ev_f1(m, b, blk=blk):
                    fc = blk * 4 + m
                    s = SCR[fc % 2]
                    act(s.t[:, 0:T], PS[b][:, 0:T], AF.Relu, [PSR[b], VEC.r[0]], s.R, bias=vcol(l, V_BF1, fc))
                    dve_tt(HB.t[:, fc, 0:T], s.t[:, 0:T], s.t[:, 0:T], ALU.mult, s.R, [HB.r[fc]])

                proj(ctx, slot, 4, XB.t, XB.r, ev_f1)
            for c in range(2):
                banks = [nps() for _ in range(4)]
                for kb in range(4):
                    slot = w_next(f"f2_{c}_{kb}")
                    for m in range(4):
                        for k in range(NK):
                            fc = kb * 8 + k
                            mm(PS[banks[m]][:, 0:T], slot.t[:, k, m * 128:(m + 1) * 128], HB.t[:, fc, 0:T],
                               kb == 0 and k == 0, kb == 3 and k == NK - 1, [slot.r[0], HB.r[fc]], [PSR[banks[m]]])
                for m in range(4):
                    kc = c * 4 + m
                    b = banks[m]
                    dve_stt(X32.t[:, kc, 0:T], X32.t[:, kc, 0:T], ALPHA, PS[b][:, 0:T], ALU.mult, ALU.add, [X32.r[kc], PSR[b]], [X32.r[kc]])
                    act(X32.t[:, kc, 0:T], X32.t[:, kc, 0:T], AF.Identity, [X32.r[kc], VEC.r[0]], [X32.r[kc]], bias=vcol(l, V_BF2, kc))
            ln_fm(ctx, l, V_LN2G, V_LN2B)

        def prompt_ctx(ti):
            c = Ctx()
            c.T, c.NS, c.L, c.C, c.NCH = TT, 1, TT, PC, TT // PC
            c.NRG, c.RGN = TT // 128, 128
            c.first = (ti == 0)
            c.last = (ti == NT - 1)
            c.x_d, c.y_d, c.tok0 = xp_d, yp_d, ti * TT
            c.pool_out = lambda l, s: pp_d[l]
            c.conv_out = lambda l, s: cp_d[l]
            c.delta_out = lambda l, s: dp_d[l]
            c.sample = False
            return c

        def sample_ctx():
            c = Ctx()
            c.T, c.NS, c.L, c.C, c.NCH = 64, 2, 32, 32, 2
            c.NRG, c.RGN = 1, 64
            c.first = False
            c.last = True
            c.x_d, c.y_d, c.tok0 = xs_d, ys_d, 0
            c.pool_out = lambda l, s: psm_d[l, s]
            c.conv_out = lambda l, s: csm_d[l, s]
            c.delta_out = lambda l, s: dsm_d[l, s]
            c.sample = True
            return c

        def run_sample():
            ctx = sample_ctx()
            load_input(ctx)
            for l in range(DEPTH):
                fm_state_load(ctx, l)
                if STOP >= 1:
                    layer(ctx, l)
                state_store(ctx, l)
            store_output(ctx)

        do_sample = with_sample and not SKIP_SAMPLE
        if do_sample and not SAMPLE_LAST:
            run_sample()
        for l in range(DEPTH):
            P.add("dve", lambda h, l=l: h.memset(PH.t[:, l], 0.0), [], [PH.r[l]])
            P.add("dve", lambda h, l=l: h.memset(CH.t[:, l], 0.0), [], [CH.r[l]])
            P.add("dve", lambda h, l=l: h.memset(S32.t[:, l], 0.0), [], [S32.r[l * 8 + h] for h in range(8)])
            P.add("dve", lambda h, l=l: h.memset(SBF.t[:, l], 0.0), [], [SBF.r[l * 8 + h] for h in range(8)])
        for ti in range(NT if not SKIP_PROMPT else 0):
            ctx = prompt_ctx(ti)
            load_input(ctx)
            for l in range(DEPTH):
                layer(ctx, l)
                if ctx.last:
                    state_store(ctx, l)
            store_output(ctx)
        if do_sample and SAMPLE_LAST:
            run_sample()
        info = P.emit(st)
        info["sbuf_bytes"] = Buf.total
        build_program.info = info
    return nc


def make_consts():
    c = np.zeros((128, NCST), np.float32)
    i = np.arange(128)
    c[:, C_ID:C_ID + 128] = np.eye(128, dtype=np.float32)
    c[:, C_TRI:C_TRI + 128] = (i[:, None] <= i[None, :]).astype(np.float32)
    c[:, C_NEGM:C_NEGM + 128] = np.where(i[None, :] >= i[:, None], 0.0, -30000.0).astype(np.float32)
    c[:, C_STR:C_STR + 128] = (i[None, :] > i[:, None]).astype(np.float32)
    c[:, C_ONE:C_ONE + 128] = 1.0
    for g in range(4):
        w = 2 << g
        t = np.arange(16)
        c[:, C_CORR + g * 16:C_CORR + (g + 1) * 16] = (w / np.minimum(w, t + 1)).astype(np.float32)[None, :]
    return c


def pack_vecs(inp):
    v = np.zeros((128, NV), np.float32)
    fm = lambda a: np.ascontiguousarray(np.asarray(a, np.float32).reshape(-1, 128).T)
    for l in range(DEPTH):
        o = l * LV
        v[:, o + V_LN1G:o + V_LN1G + 8] = fm(inp["ln1_g"][l])
        v[:, o + V_LN1B:o + V_LN1B + 8] = fm(inp["ln1_b"][l])
        v[:, o + V_LN2G:o + V_LN2G + 8] = fm(inp["ln2_g"][l])
        v[:, o + V_LN2B:o + V_LN2B + 8] = fm(inp["ln2_b"][l])
        v[:, o + V_BF2:o + V_BF2 + 8] = fm(inp["b_ff2"][l])
        v[:, o + V_PSC:o + V_PSC + 8] = fm(inp["pool_scale"][l])
        v[:, o + V_BF1:o + V_BF1 + 32] = fm(inp["b_ff1"][l])
        for j in range(4):
            v[:, o + V_CW + j * 24:o + V_CW + (j + 1) * 24] = fm(inp["conv_w"][l, j])
        v[:, o + V_OG] = np.asarray(inp["o_gain"][l], np.float32)
    v[:, V_ING:V_ING + 8] = fm(inp["ln_in_g"])
    v[:, V_INB:V_INB + 8] = fm(inp["ln_in_b"])
    return v


_CACHE = {}


def run(inp, SEQ=SEQ_FULL, with_sample=True, trace=False):
    key = (SEQ, with_sample)
    if key not in _CACHE:
        _CACHE[key] = build_program(SEQ, with_sample)
    nc = _CACHE[key]
    f = lambda a: np.ascontiguousarray(np.asarray(a, np.float32))
    cst = make_consts()
    vecs = pack_vecs(inp)
    bc = np.zeros((128, 32), np.float32)
    for l in range(DEPTH):
        bc[:, l * 16:l * 16 + 8] = np.asarray(inp["a_log"][l], np.float32)[None, :]
        bc[:, l * 16 + 8:l * 16 + 16] = np.asarray(inp["dt_bias"][l], np.float32)[None, :]
    shared = {"w_in": f(inp["w_in"]), "w_pool": f(inp["w_pool"]), "w_out": f(inp["w_out"]), "w_ff1": f(inp["w_ff1"]),
              "w_ff2": f(inp["w_ff2"]), "vecs": vecs, "bc": bc, "cst": cst}
    in_maps = []
    for c in range(8):
        m = dict(shared)
        m["xp"] = f(inp["x_prompt"][c, :SEQ])
        m["xs"] = f(inp["x_sample"][2 * c:2 * c + 2]).reshape(64, D)
        m["st_pool"] = f(inp["state_pool"][:, 2 * c:2 * c + 2])
        m["st_conv"] = f(inp["state_conv"][:, 2 * c:2 * c + 2])
        m["st_delta"] = f(inp["state_delta"][:, 2 * c:2 * c + 2])
        in_maps.append(m)
    res = run_bass_kernel_spmd(nc, in_maps, core_ids=list(range(8)), **({"trace": True} if trace else {}))
    R = res.results
    yp = np.stack([R[c]["yp"] for c in range(8)], 0)
    ys = np.concatenate([R[c]["ys"].reshape(2, 32, D) for c in range(8)], 0)
    pool_p = np.stack([R[c]["pool_p"] for c in range(8)], 1)
    conv_p = np.stack([R[c]["conv_p"] for c in range(8)], 1)
    delta_p = np.stack([R[c]["delta_p"] for c in range(8)], 1)
    pool_s = np.concatenate([R[c]["pool_s"] for c in range(8)], 1)
    conv_s = np.concatenate([R[c]["conv_s"] for c in range(8)], 1)
    delta_s = np.concatenate([R[c]["delta_s"] for c in range(8)], 1)
    outs = (yp, ys, pool_p, conv_p, delta_p, pool_s, conv_s, delta_s)
    return tuple(np.ascontiguousarray(o, dtype=np.float32) for o in outs), res


def kernel(**inputs):
    outs, _ = run(inputs)
    return outs
```

```python
import numpy as np
from contextlib import ExitStack
import concourse.bass as bass
import concourse.mybir as mybir
from concourse.bass_utils import run_bass_kernel_spmd

F32 = mybir.dt.float32
BF16 = mybir.dt.bfloat16
AF = mybir.ActivationFunctionType
ALU = mybir.AluOpType
SEM_LIM = 8000
import os
STOP = int(os.environ.get("KSTOP", "9"))
SKIP_PROMPT = int(os.environ.get("KSKIP_PROMPT", "0"))
SKIP_SAMPLE = int(os.environ.get("KSKIP_SAMPLE", "0"))
KDL = int(os.environ.get("KDL", "0"))
NSET = int(os.environ.get("KNSET", "4"))
WSCR = int(os.environ.get("KWSCR", "1"))
BF_INV = int(os.environ.get("KBFINV", "0"))
NACC = int(os.environ.get("KNACC", "4"))
WQ2 = int(os.environ.get("KWQ2", "1"))
POOLOFF = int(os.environ.get("KPOOLOFF", "1"))
OVL = int(os.environ.get("KOVL", "0"))
SAMPLE_LAST = int(os.environ.get("KSLAST", "1"))
NFILL = int(os.environ.get("KNFILL", "0"))
FILLN = int(os.environ.get("KFILLN", "256"))
FP32R = int(os.environ.get("KFP32R", "0"))
PC = int(os.environ.get("KPC", "128"))
TT = int(os.environ.get("KTT", "512"))

D = 1024
NK = 8
DEPTH = 2
SEQ_FULL = 4096
IN_W = 7184
ALPHA = (2 * DEPTH) ** 0.25
LN_EPS = 1e-5
RMS_EPS = 1e-6
L2_EPS = 1e-6
LV = 180
V_LN1G, V_LN1B, V_LN2G, V_LN2B, V_BF2, V_PSC, V_BF1, V_CW, V_OG = 0, 8, 16, 24, 32, 40, 48, 80, 176
V_ING, V_INB = 2 * LV, 2 * LV + 8
NV = 2 * LV + 16
C_ID, C_TRI, C_NEGM, C_STR, C_ONE, C_CORR = 0, 128, 256, 384, 512, 640
NCST = 640 + 64


class Reg:
    __slots__ = ("name", "w", "rs")

    def __init__(self, name):
        self.name = name
        self.w = None
        self.rs = []


class Ins:
    __slots__ = ("eng", "fn", "deps", "ticket", "need", "dsem", "dticket", "idx")


class DSem:
    __slots__ = ("name", "count", "sem")

    def __init__(self, name):
        self.name = name
        self.count = 0
        self.sem = None


class Prog:
    ENGS = ("pe", "act", "dve", "pool", "sp")

    def __init__(self, nc):
        self.nc = nc
        self.ins = []
        self.out_dmas = []
        self.dsems = []

    def dsem(self, name):
        d = DSem(name)
        self.dsems.append(d)
        return d

    def add(self, eng, fn, reads=(), writes=(), dsem=None, is_out=False):
        i = Ins()
        i.eng = eng
        i.fn = fn
        i.idx = len(self.ins)
        i.need = False
        i.ticket = None
        i.dsem = dsem
        i.dticket = None
        deps = set()
        for r in reads:
            if r.w is not None:
                deps.add(r.w)
        for w in writes:
            if w.w is not None:
                deps.add(w.w)
            deps.update(w.rs)
        for r in reads:
            if dsem is None:
                r.rs = [x for x in r.rs if not (self.ins[x].eng == eng and self.ins[x].dsem is None)]
            r.rs.append(i.idx)
        for w in writes:
            w.w = i.idx
            w.rs = []
        deps.discard(i.idx)
        i.deps = sorted(deps)
        if dsem is not None:
            dsem.count += 1
            i.dticket = dsem.count
        for d in i.deps:
            self.ins[d].need = True
        self.ins.append(i)
        if is_out:
            self.out_dmas.append(i.idx)
        return i

    def fence(self, src_regs, dst_regs):
        pend = []
        for s in src_regs:
            if s.w is not None:
                pend.append(s.w)
            pend.extend(s.rs)
        for d in dst_regs:
            d.rs = list(set(d.rs) | set(pend))

    def emit(self, stack):
        nc = self.nc
        fin = Ins()
        fin.eng = "sp"
        fin.fn = None
        fin.idx = len(self.ins)
        fin.need = False
        fin.dsem = None
        fin.deps = list(self.out_dmas)
        fin.ticket = None
        fin.dticket = None
        self.ins.append(fin)
        cnt = {e: 0 for e in self.ENGS}
        for i in self.ins:
            if i.dsem is None and i.need:
                cnt[i.eng] += 1
                i.ticket = cnt[i.eng]
        esems = {}
        for e in self.ENGS:
            n = (cnt[e] + SEM_LIM - 1) // SEM_LIM
            esems[e] = [stack.enter_context(nc.semaphore(f"s_{e}{k}")) for k in range(max(n, 1))]
        for d in self.dsems:
            if d.count > 0:
                d.sem = stack.enter_context(nc.semaphore(f"d_{d.name}"))
        per_eng = {e: [i for i in self.ins if i.eng == e] for e in self.ENGS}
        ins_all = self.ins
        nwaits = [0]

        def run_engine(e, h):
            wm = {}
            maxk = {}
            for i in per_eng[e]:
                for d in i.deps:
                    di = ins_all[d]
                    if di.dsem is not None:
                        key = ("d", id(di.dsem))
                        val = di.dticket * 16
                        sem = di.dsem.sem
                    else:
                        if di.eng == e and e == "pe":
                            continue
                        k = (di.ticket - 1) // SEM_LIM
                        if maxk.get(di.eng, -1) > k:
                            continue
                        key = (di.eng, k)
                        val = (di.ticket - 1) % SEM_LIM + 1
                        sem = esems[di.eng][k]
                    if wm.get(key, 0) >= val:
                        continue
                    h.wait_ge(sem, val)
                    nwaits[0] += 1
                    wm[key] = val
                    if di.dsem is None:
                        maxk[di.eng] = max(maxk.get(di.eng, -1), key[1])
                if i.fn is None:
                    continue
                r = i.fn(h)
                if i.dsem is not None:
                    r.then_inc(i.dsem.sem, 16)
                elif i.need:
                    k = (i.ticket - 1) // SEM_LIM
                    r.then_inc(esems[e][k], 1)

        block = stack.enter_context(nc.Block())

        @block.tensor
        def _(h):
            run_engine("pe", h)

        @block.scalar
        def _(h):
            run_engine("act", h)

        @block.vector
        def _(h):
            run_engine("dve", h)

        @block.gpsimd
        def _(h):
            run_engine("pool", h)

        @block.sync
        def _(h):
            run_engine("sp", h)

        return dict(n_ins=len(self.ins), n_waits=nwaits[0], cnt=cnt)


class Buf:
    total = 0
    sizes = []

    def __init__(self, P, st, nc, name, shape, dtype, nreg=1):
        self.t = st.enter_context(nc.sbuf_tensor(name, shape, dtype))
        nb = int(np.prod(shape[1:])) * (2 if dtype == BF16 else 4)
        Buf.total += (nb + 31) // 32 * 32
        Buf.sizes.append((name, nb))
        self.r = [Reg(f"{name}{i}") for i in range(nreg)]
        self.ds = [None] * nreg
        self.P = P
        self.name = name

    @property
    def R(self):
        return self.r

    def dsem(self, i=0):
        if self.ds[i] is None:
            self.ds[i] = self.P.dsem(f"{self.name}{i}")
        return self.ds[i]


def build_program(SEQ, with_sample=True):
    nc = bass.Bass("TRN2", target_bir_lowering=False)
    NT = SEQ // TT
    dr = lambda n, s, k: nc.dram_tensor(n, s, F32, kind=k).ap()
    xp_d = dr("xp", [SEQ, D], "ExternalInput")
    xs_d = dr("xs", [64, D], "ExternalInput")
    stp_d = dr("st_pool", [DEPTH, 2, 15, D], "ExternalInput")
    stc_d = dr("st_conv", [DEPTH, 2, 3, 3072], "ExternalInput")
    std_d = dr("st_delta", [DEPTH, 2, 8, 128, 128], "ExternalInput")
    w_in_d = dr("w_in", [DEPTH, D, IN_W], "ExternalInput")
    w_pool_d = dr("w_pool", [DEPTH, 4, 256, 256], "ExternalInput")
    w_out_d = dr("w_out", [DEPTH, D, D], "ExternalInput")
    w_ff1_d = dr("w_ff1", [DEPTH, D, 4096], "ExternalInput")
    w_ff2_d = dr("w_ff2", [DEPTH, 4096, D], "ExternalInput")
    vecs_d = dr("vecs", [128, NV], "ExternalInput")
    bc_d = dr("bc", [128, 32], "ExternalInput")
    cst_d = dr("cst", [128, NCST], "ExternalInput")
    yp_d = dr("yp", [SEQ, D], "ExternalOutput")
    ys_d = dr("ys", [64, D], "ExternalOutput")
    pp_d = dr("pool_p", [DEPTH, 15, D], "ExternalOutput")
    cp_d = dr("conv_p", [DEPTH, 3, 3072], "ExternalOutput")
    dp_d = dr("delta_p", [DEPTH, 8, 128, 128], "ExternalOutput")
    psm_d = dr("pool_s", [DEPTH, 2, 15, D], "ExternalOutput")
    csm_d = dr("conv_s", [DEPTH, 2, 3, 3072], "ExternalOutput")
    dsm_d = dr("delta_s", [DEPTH, 2, 8, 128, 128], "ExternalOutput")

    P = Prog(nc)
    st = ExitStack()
    with st:
        mk = lambda name, shape, dt=F32, nreg=1: Buf(P, st, nc, name, shape, dt, nreg)
        CST = mk("CST", [128, NCST])
        VEC = mk("VEC", [128, NV])
        BC = mk("BC", [128, 32])
        NEGA = mk("NEGA", [128, 16])
        ONEB = mk("ONEB", [128, 128], BF16)
        IDENT = CST.t[:, C_ID:C_ID + 128]
        TRI_B = mk("TRIB", [128, 128])
        TRI = TRI_B.t[:, :]
        ONES_B = mk("ONESB", [128, 128])
        NEGM = CST.t[:, C_NEGM:C_NEGM + 128]
        STRICT = CST.t[:, C_STR:C_STR + 128]
        ONES = ONES_B.t[:, :]
        ONESN = mk("ONESN", [128, 128])
        X32 = mk("X32", [128, NK, TT], F32, NK)
        XB = mk("XB", [128, NK, TT], BF16, NK)
        XIN = [mk(f"XIN{i}", [128, D]) for i in range(2)]
        STAT = mk("STAT", [128, 16])
        UE = [mk(f"UE{i}", [128, 16 + TT]) for i in range(2)]
        SA = mk("SA", [128, 16 + TT])
        SB_ = mk("SB", [128, 16 + TT])
        PH = mk("PH", [128, DEPTH, NK, 2, 16], F32, DEPTH)
        YA = mk("YA", [128, NK, TT], BF16, NK)
        SCR = [mk(f"SCR{i}", [128, TT]) for i in range(NACC)]
        CE = [mk(f"CE{i}", [128, 4 + TT]) for i in range(2)]
        ACC = [mk(f"ACC{i}", [128, TT]) for i in range(NACC)]
        CH = mk("CH", [128, DEPTH, 24, 2, 4], F32, DEPTH)
        HB = mk("HB", [128, 32, TT], BF16, 32)
        QT_t = HB.t[:, 0:8, :]
        KT_t = HB.t[:, 8:16, :]
        QKV = mk("QKVR", [1, 4], F32, 16)
        KTM = mk("KTM", [128, TT // PC, D], BF16, 1)
        VTM = mk("VTM", [128, TT // PC, D], BF16, 1)
        GZ = mk("GZ", [128, NK, TT], BF16, NK)
        MIX = GZ
        OT = XB
        RS = mk("RS", [128, TT])
        RS2 = mk("RS2", [128, TT])
        S32 = mk("S32", [128, DEPTH, 8, 128], F32, DEPTH * 8)
        SBF = mk("SBF", [128, DEPTH, 8, 128], BF16, DEPTH * 8)
        BETA = mk("BETA", [128, TT // PC, 8])
        GG = mk("GG", [128, TT // PC, 8])
        BAT = mk("BAT", [128, 8])
        WBA = mk("WBA", [128, NK, 16], BF16)
        STG = mk("STG", [16, D])
        STG2 = mk("STG2", [4, 512])
        NSLOT = 3
        WS = [mk(f"W{i}", [128, NK, 512], BF16) for i in range(NSLOT)]
        dl = []
        for s in range(NSET):
            d_ = {}
            for n in ("GH", "EE"):
                d_[n] = mk(f"{n}{s}", [128, PC])
            IDT = BF16 if BF_INV else (mybir.dt.float32r if FP32R else F32)
            d_["PX"] = mk(f"PX{s}", [128, PC], IDT)
            for n in ("AAa", "AAb"):
                d_[n] = mk(f"{n}{s}", [128, 2, PC], IDT)
            for n in ("ATT", "KD", "QD") + (() if BF_INV else ("NTB",)):
                d_[n] = mk(f"{n}{s}", [128, PC], BF16)
            for n in ("RP", "VNB", "KW"):
                d_[n] = mk(f"{n}{s}", [128, 128], BF16)
            dl.append(d_)
        pc = []
        for s in range(2):
            d_ = {}
            for n in ("GCC", "NGCC", "GL", "WJ", "SDEC", "TW"):
                d_[n] = mk(f"{n}{s}", [128, 8])
            pc.append(d_)
        PS = [st.enter_context(nc.psum_tensor(f"ps{i}", [128, 512], F32)) for i in range(8)]
        PSR = [Reg(f"ps{i}") for i in range(8)]
        psi = [0]

        def nps():
            b = psi[0] % (7 if NFILL else 8)
            psi[0] += 1
            return b

        def dve_tt(out, a, b, op, R, W):
            P.add("dve", lambda h: h.tensor_tensor(out=out, in0=a, in1=b, op=op), R, W)

        def dve_ts(out, a, s1, s2, op0, op1, R, W):
            if op1 is None:
                P.add("dve", lambda h: h.tensor_scalar(out=out, in0=a, scalar1=s1, scalar2=None, op0=op0), R, W)
            else:
                P.add("dve", lambda h: h.tensor_scalar(out=out, in0=a, scalar1=s1, scalar2=s2, op0=op0, op1=op1), R, W)

        def dve_stt(out, a, s, b, op0, op1, R, W):
            P.add("dve", lambda h: h.scalar_tensor_tensor(out=out, in0=a, scalar=s, in1=b, op0=op0, op1=op1), R, W)

        def pool_cp(out, a, R, W):
            P.add("pool", lambda h: h.tensor_copy(out=out, in_=a), R, W)

        def dve_cp(out, a, R, W):
            P.add("dve", lambda h: h.tensor_copy(out=out, in_=a), R, W)

        def act(out, a, func, R, W, bias=None, scale=None):
            kw = {}
            if bias is not None:
                kw["bias"] = bias
            if scale is not None:
                kw["scale"] = scale
            P.add("act", lambda h: h.activation(out=out, in_=a, func=func, **kw), R, W)

        def mm(out, lhsT, rhs, start, stop, R, W):
            P.add("pe", lambda h: h.matmul(out, lhsT, rhs, start=start, stop=stop), R, W)

        def tr(out, in_, n, R, W):
            P.add("pe", lambda h: h.transpose(out, in_, IDENT[0:n, 0:n]), R + [CST.r[0]], W)

        def dma(q, out, in_, ds, R, W, is_out=False):
            P.add(q, lambda h: h.dma_start(out=out, in_=in_), R, W, dsem=ds, is_out=is_out)

        def vcol(l, off, k=0):
            c = l * LV + off + k
            return VEC.t[:, c:c + 1]

        def layer_blocks(l):
            bl = []
            kp = lambda ap: ap.rearrange("(k p) c -> p k c", p=128)
            for i in range(2):
                bl.append((f"up{i}", kp(w_in_d[l, :, i * 512:(i + 1) * 512]), 512))
            bl.append(("wp", w_pool_d[l].rearrange("g (c p) d -> p (g c) d", p=128), 256))
            for i in range(2):
                bl.append((f"ga{i}", kp(w_in_d[l, :, 5120 + i * 512:5120 + (i + 1) * 512]), 512))
            for i in range(6):
                bl.append((f"qkv{i}", kp(w_in_d[l, :, 1024 + i * 512:1024 + (i + 1) * 512]), 512))
            for i in range(2):
                bl.append((f"z{i}", kp(w_in_d[l, :, 4096 + i * 512:4096 + (i + 1) * 512]), 512))
            for i in range(2):
                bl.append((f"gb{i}", kp(w_in_d[l, :, 6144 + i * 512:6144 + (i + 1) * 512]), 512))
            for i in range(2):
                bl.append((f"wo{i}", kp(w_out_d[l, :, i * 512:(i + 1) * 512]), 512))
            for i in range(8):
                bl.append((f"f1_{i}", kp(w_ff1_d[l, :, i * 512:(i + 1) * 512]), 512))
            for c in range(2):
                for k in range(4):
                    bl.append((f"f2_{c}_{k}", kp(w_ff2_d[l, k * 1024:(k + 1) * 1024, c * 512:(c + 1) * 512]), 512))
            return bl

        npass = NT + (1 if with_sample else 0)
        allblocks = []
        for _ in range(npass):
            for l in range(DEPTH):
                allblocks.extend(layer_blocks(l))
        wstate = dict(issued=0, used=0)

        NB = 2 * len(layer_blocks(0))
        wscr = nc.dram_tensor("wscr", [NB, 128, NK * 512], BF16, kind="Internal").ap()
        use_scr = (STOP >= 9 and not SKIP_PROMPT and not SKIP_SAMPLE and WSCR)

        def w_issue():
            i = wstate["issued"]
            if i >= len(allblocks):
                return
            name, src, ncol = allblocks[i]
            s = WS[i % NSLOT]
            j = i % NB
            scr = wscr[j, :, 0:NK * ncol].rearrange("p (k c) -> p k c", k=NK)
            if not hasattr(s, "hwds"):
                s.hwds = P.dsem(f"{s.name}hw")
            if use_scr and i >= NB:
                if WQ2 and i % 2 == 0:
                    dma("sp", s.t[:, :, 0:ncol], scr, s.hwds, [], s.R)
                else:
                    dma("pool", s.t[:, :, 0:ncol], scr, s.dsem(), [], s.R)
            else:
                dma("pool", s.t[:, :, 0:ncol], src, s.dsem(), [], s.R)
                if use_scr:
                    dma("sp", scr, s.t[:, :, 0:ncol], s.hwds, s.R, [])
            wstate["issued"] += 1

        def w_next(name):
            i = wstate["used"]
            if STOP < 9 or SKIP_PROMPT or SKIP_SAMPLE:
                while allblocks[i][0] != name:
                    del allblocks[i]
            assert allblocks[i][0] == name, (allblocks[i][0], name)
            while wstate["issued"] < min(i + NSLOT, len(allblocks)):
                w_issue()
            wstate["used"] += 1
            return WS[i % NSLOT]

        dma("sp", CST.t[:], cst_d, CST.dsem(), [], CST.R)
        dma("sp", VEC.t[:], vecs_d, VEC.dsem(), [], VEC.R)
        dma("sp", BC.t[:], bc_d, BC.dsem(), [], BC.R)
        for l in range(DEPTH):
            act(NEGA.t[:, l * 8:(l + 1) * 8], BC.t[:, l * 16:l * 16 + 8], AF.Exp, BC.R, NEGA.R)
        dve_ts(NEGA.t[:], NEGA.t[:], -1.0, None, ALU.mult, None, NEGA.R, NEGA.R)
        dve_cp(TRI_B.t[:], CST.t[:, C_TRI:C_TRI + 128], CST.R, CST.R)
        dve_cp(ONES_B.t[:], CST.t[:, C_ONE:C_ONE + 128], CST.R, CST.R)
        dve_cp(ONEB.t[:], ONES, CST.R, ONEB.R)
        dve_ts(ONESN.t[:], ONES, 1.0 / D, None, ALU.mult, None, CST.R, ONESN.R)

        for bf in UE + CE + [SA, SB_]:
            P.add("dve", lambda h, bf=bf: h.memset(bf.t[:], 0.0), [], bf.R)
        class Ctx:
            pass

        def ln_fm(ctx, l, goff, boff):
            T = ctx.T
            b = nps()
            for k in range(NK):
                mm(PS[b][:, 0:T], ONESN.t[:], X32.t[:, k, 0:T], k == 0, k == NK - 1, [ONESN.r[0], X32.r[k]], [PSR[b]])
            for k in range(NK):
                dve_tt(X32.t[:, k, 0:T], X32.t[:, k, 0:T], PS[b][:, 0:T], ALU.subtract, [X32.r[k], PSR[b]], [X32.r[k]])
            b2 = nps()
            for k in range(NK):
                s = SCR[k % 2]
                act(s.t[:, 0:T], X32.t[:, k, 0:T], AF.Square, [X32.r[k]], s.R)
                mm(PS[b2][:, 0:T], ONESN.t[:], s.t[:, 0:T], k == 0, k == NK - 1, [ONESN.r[0], s.r[0]], [PSR[b2]])
            act(RS2.t[:, 0:T], PS[b2][:, 0:T], AF.Ln, [PSR[b2], EPSB.r[0]], RS2.R, bias=EPS_LN)
            act(RS.t[:, 0:T], RS2.t[:, 0:T], AF.Exp, RS2.R, RS.R, scale=-0.5)
            for k in range(NK):
                dve_stt(X32.t[:, k, 0:T], X32.t[:, k, 0:T], vcol(l, goff, k) if l >= 0 else None, RS.t[:, 0:T],
                        ALU.mult, ALU.mult, [X32.r[k], RS.r[0], VEC.r[0]], [X32.r[k]])
                act(X32.t[:, k, 0:T], X32.t[:, k, 0:T], AF.Identity, [X32.r[k], VEC.r[0]], [X32.r[k]],
                    bias=vcol(l, boff, k))
                if POOLOFF:
                    pool_cp(XB.t[:, k, 0:T], X32.t[:, k, 0:T], [X32.r[k]], [XB.r[k]])
                else:
                    act(XB.t[:, k, 0:T], X32.t[:, k, 0:T], AF.Copy, [X32.r[k]], [XB.r[k]])

        def proj(ctx, slot, nchunk, rhs, rhs_regs, evac, coff=0):
            T = ctx.T
            for m in range(nchunk):
                b = nps()
                for k in range(NK):
                    mm(PS[b][:, 0:T], slot.t[:, k, coff + m * 128:coff + (m + 1) * 128], rhs[:, k, 0:T], k == 0, k == NK - 1,
                       [slot.r[0], rhs_regs[k]], [PSR[b]])
                evac(m, b)

        EPSB = mk("EPSB", [128, 4])
        P.add("dve", lambda h: h.memset(EPSB.t[:, 0:1], LN_EPS), [], EPSB.R)
        P.add("dve", lambda h: h.memset(EPSB.t[:, 1:2], RMS_EPS), [], EPSB.R)
        P.add("dve", lambda h: h.memset(EPSB.t[:, 2:3], L2_EPS), [], EPSB.R)
        P.add("dve", lambda h: h.memset(EPSB.t[:, 3:4], 1.0), [], EPSB.R)
        EPS_LN = EPSB.t[:, 0:1]
        EPS_RMS = EPSB.t[:, 1:2]
        EPS_L2 = EPSB.t[:, 2:3]
        ONE_B = EPSB.t[:, 3:4]

        def load_input(ctx):
            T = ctx.T
            for r in range(ctx.NRG):
                n = ctx.RGN
                xi = XIN[r % 2]
                dma("sp", xi.t[0:n, :], ctx.x_d[ctx.tok0 + r * n: ctx.tok0 + (r + 1) * n, :], xi.dsem(), [], xi.R)
                P.add("dve", lambda h, xi=xi, n=n: h.bn_stats(out=STAT.t[0:n, 0:6], in_=xi.t[0:n, 0:512]), xi.R, STAT.R)
                P.add("dve", lambda h, xi=xi, n=n: h.bn_stats(out=STAT.t[0:n, 6:12], in_=xi.t[0:n, 512:1024]), xi.R, STAT.R)
                P.add("dve", lambda h, n=n: h.bn_aggr(out=STAT.t[0:n, 12:14], in_=STAT.t[0:n, 0:12]), STAT.R, STAT.R)
                act(STAT.t[0:n, 14:15], STAT.t[0:n, 13:14], AF.Ln, STAT.R + EPSB.R, STAT.R, bias=EPS_LN[0:n])
                act(STAT.t[0:n, 15:16], STAT.t[0:n, 14:15], AF.Exp, STAT.R, STAT.R, scale=-0.5)
                dve_ts(xi.t[0:n, :], xi.t[0:n, :], STAT.t[0:n, 12:13], STAT.t[0:n, 15:16], ALU.subtract, ALU.mult,
                       xi.R + STAT.R, xi.R)
                for half in range(2):
                    b = nps()
                    for kk in range(4):
                        k = half * 4 + kk
                        tr(PS[b][:, kk * 128:kk * 128 + n], xi.t[0:n, k * 128:(k + 1) * 128], n, xi.R, [PSR[b]])
                    for kk in range(4):
                        k = half * 4 + kk
                        act(X32.t[:, k, r * n:(r + 1) * n], PS[b][:, kk * 128:kk * 128 + n], AF.Identity,
                            [PSR[b], VEC.r[0]], [X32.r[k]], bias=VEC.t[:, V_INB + k:V_INB + k + 1],
                            scale=VEC.t[:, V_ING + k:V_ING + k + 1])
                        dve_cp(XB.t[:, k, r * n:(r + 1) * n], X32.t[:, k, r * n:(r + 1) * n], [X32.r[k]], [XB.r[k]])

        def store_output(ctx):
            T = ctx.T
            for r in range(ctx.NRG):
                n = ctx.RGN
                xi = XIN[r % 2]
                for half in range(2):
                    b = nps()
                    for kk in range(4):
                        k = half * 4 + kk
                        tr(PS[b][0:n, kk * 128:(kk + 1) * 128], X32.t[:, k, r * n:(r + 1) * n], 128, [X32.r[k]], [PSR[b]])
                    act(xi.t[0:n, half * 512:(half + 1) * 512], PS[b][0:n, :], AF.Copy, [PSR[b]], xi.R)
                dma("sp", ctx.y_d[ctx.tok0 + r * n: ctx.tok0 + (r + 1) * n, :], xi.t[0:n, :], xi.dsem(), xi.R, [], is_out=True)

        def fm_state_load(ctx, l):
            for s in range(ctx.NS):
                dma("sp", STG.t[0:15, :], stp_d[l, s], STG.dsem(), [], STG.R)
                b = nps()
                for k in range(NK):
                    tr(PS[b][:, k * 16:k * 16 + 15], STG.t[0:15, k * 128:(k + 1) * 128], 15, STG.R, [PSR[b]])
                dve_cp(PH.t[:, l, :, s, 1:16], PS[b][:, 0:128].rearrange("p (k c) -> p k c", c=16)[:, :, 0:15], [PSR[b]], [PH.r[l]])
                for g in range(6):
                    dma("sp", STG2.t[0:3, :], stc_d[l, s, :, g * 512:(g + 1) * 512], STG2.dsem(), [], STG2.R)
                    b = nps()
                    for m in range(4):
                        tr(PS[b][:, m * 4:m * 4 + 3], STG2.t[0:3, m * 128:(m + 1) * 128], 3, STG2.R, [PSR[b]])
                    dve_cp(CH.t[:, l, g * 4:(g + 1) * 4, s, 1:4], PS[b][:, 0:16].rearrange("p (k c) -> p k c", c=4)[:, :, 0:3],
                           [PSR[b]], [CH.r[l]])

        def state_store(ctx, l):
            for s in range(ctx.NS):
                for half in range(2):
                    b = nps()
                    for kk in range(4):
                        k = half * 4 + kk
                        tr(PS[b][0:15, kk * 128:(kk + 1) * 128], PH.t[:, l, k, s, 1:16], 128, [PH.r[l]], [PSR[b]])
                    act(STG.t[0:15, half * 512:(half + 1) * 512], PS[b][0:15, :], AF.Copy, [PSR[b]], STG.R)
                dma("sp", ctx.pool_out(l, s), STG.t[0:15, :], STG.dsem(), STG.R, [], is_out=True)
                for g in range(6):
                    b = nps()
                    for mm_ in range(4):
                        m = g * 4 + mm_
                        tr(PS[b][0:3, mm_ * 128:(mm_ + 1) * 128], CH.t[:, l, m, s, 1:4], 128, [CH.r[l]], [PSR[b]])
                    act(STG2.t[0:3, :], PS[b][0:3, :], AF.Copy, [PSR[b]], STG2.R)
                    dma("sp", ctx.conv_out(l, s)[:, g * 512:(g + 1) * 512], STG2.t[0:3, :], STG2.dsem(), STG2.R, [], is_out=True)

        def to_tm(dst, hh, src_t, src_R, ctx, dst_regs):
            C, NCH = ctx.C, ctx.NCH
            for g0 in range(0, NCH, 4):
                n = min(4, NCH - g0)
                bb = nps()
                for j in range(n):
                    ci = g0 + j
                    tr(PS[bb][0:C, j * 128:(j + 1) * 128], src_t[:, ci * C:(ci + 1) * C], 128, src_R, [PSR[bb]])
                act(dst.t[0:C, g0:g0 + n, hh * 128:(hh + 1) * 128], PS[bb][0:C, 0:n * 128].rearrange("p (a b) -> p a b", b=128),
                    AF.Copy, [PSR[bb]], dst_regs)

        def delta_phase(ctx, l):
            T, NS, L, C, NCH = ctx.T, ctx.NS, ctx.L, ctx.C, ctx.NCH
            K = {128: 6, 64: 5, 32: 4}[C]
            Sregs = [S32.r[l * 8 + h] for h in range(8)]
            SBregs = [SBF.r[l * 8 + h] for h in range(8)]

            pcnt = [0, 0]

            def nps_c():
                if not OVL:
                    return nps()
                pcnt[0] += 1
                return (pcnt[0] - 1) % 4

            def nps_p():
                if not OVL:
                    return nps()
                pcnt[1] += 1
                return 4 + (pcnt[1] - 1) % 4

            def perchunk(ci):
                pcs = pc[ci % 2]
                b = nps()
                mm(PS[b][0:C, 0:8], TRI[0:C, 0:C], GG.t[0:C, ci, :], True, True, CST.R + GG.R, [PSR[b]])
                mm(PS[b][:, 8:16], ONES[0:C, :], GG.t[0:C, ci, :], True, True, CST.R + GG.R, [PSR[b]])
                dve_cp(pcs["GCC"].t[0:C, :], PS[b][0:C, 0:8], [PSR[b]], pcs["GCC"].R)
                dve_ts(pcs["NGCC"].t[0:C, :], PS[b][0:C, 0:8], -1.0, None, ALU.mult, None, [PSR[b]], pcs["NGCC"].R)
                dve_cp(pcs["GL"].t[:, :], PS[b][:, 8:16], [PSR[b]], pcs["GL"].R)
                dve_tt(pcs["TW"].t[0:C, :], pcs["GL"].t[0:C, :], pcs["GCC"].t[0:C, :], ALU.subtract,
                       pcs["GL"].R + pcs["GCC"].R, pcs["TW"].R)
                act(pcs["WJ"].t[0:C, :], pcs["TW"].t[0:C, :], AF.Exp, pcs["TW"].R, pcs["WJ"].R)
                act(pcs["SDEC"].t[:, :], pcs["GL"].t[:, :], AF.Exp, pcs["GL"].R, pcs["SDEC"].R)


            def pre(ci, h):
                c0 = ci * C
                pcs = pc[ci % 2]
                d = dl[h % NSET]
                G1 = d["GH"]
                qs = QT_t[:, h, c0:c0 + C]
                ks = KT_t[:, h, c0:c0 + C]
                qr, kr = QKV.r[h], QKV.r[8 + h]
                dve_ts(G1.t[0:C, 0:C], TRI[0:C, 0:C], GG.t[0:C, ci, h:h + 1], None, ALU.mult, None, CST.R + GG.R, G1.R)
                yield
                b1 = nps_p()
                mm(PS[b1][:, 0:C], ONES[0:C, :], G1.t[0:C, 0:C], True, True, CST.R + G1.R, [PSR[b1]])
                yield
                dve_tt(G1.t[0:C, 0:C], PS[b1][0:C, 0:C], NEGM[0:C, 0:C], ALU.add, [PSR[b1]] + CST.R, G1.R)
                yield
                act(d["EE"].t[:, 0:C], PS[b1][:, 0:C], AF.Exp, [PSR[b1]] + G1.R, d["EE"].R)
                act(G1.t[0:C, 0:C], G1.t[0:C, 0:C], AF.Exp, G1.R + pcs["NGCC"].R, G1.R, bias=pcs["NGCC"].t[0:C, h:h + 1])
                b2 = nps_p()
                mm(PS[b2][0:C, 0:C], ks, qs, True, True, [kr, qr], [PSR[b2]])
                mm(PS[b2][0:C, C:2 * C], ks, ks, True, True, [kr], [PSR[b2]])
                yield
                dve_tt(d["ATT"].t[0:C, 0:C], PS[b2][0:C, 0:C], G1.t[0:C, 0:C], ALU.mult, [PSR[b2]] + G1.R, d["ATT"].R)
                dve_stt(G1.t[0:C, 0:C], G1.t[0:C, 0:C], BETA.t[0:C, ci, h:h + 1], STRICT[0:C, 0:C], ALU.mult, ALU.mult,
                        G1.R + BETA.R + CST.R, G1.R)
                A, Bn = d["AAa"], d["AAb"]
                dve_tt(G1.t[0:C, 0:C], PS[b2][0:C, C:2 * C], G1.t[0:C, 0:C], ALU.mult, [PSR[b2]] + G1.R, G1.R)
                yield
                b3 = nps_p()
                tr(PS[b3][0:C, 0:C], G1.t[0:C, 0:C], C, G1.R, [PSR[b3]])
                dve_tt(d["PX"].t[0:C, 0:C], IDENT[0:C, 0:C], G1.t[0:C, 0:C], ALU.subtract, CST.R + G1.R, d["PX"].R)
                act(A.t[0:C, 0, 0:C], G1.t[0:C, 0:C], AF.Copy, G1.R, A.R)
                yield
                act(A.t[0:C, 1, 0:C], PS[b3][0:C, 0:C], AF.Copy, [PSR[b3]], A.R)
                yield
                cur, nxt = A, Bn
                rr = (lambda ap: ap) if (not FP32R or C == 128) else (lambda ap: ap.bitcast(F32))

                def squaring(src, k):
                    bq = nps_p()
                    if k < K:
                        mm(PS[bq][0:C, 0:C], rr(src.t[0:C, 1, 0:C]), rr(src.t[0:C, 0, 0:C]), True, True, src.R, [PSR[bq]])
                    mm(PS[bq][0:C, C:2 * C], rr(src.t[0:C, 0, 0:C]), rr(src.t[0:C, 1, 0:C]), True, True, src.R, [PSR[bq]])
                    return bq

                def evac_sq(dst, bq, k):
                    if k < K:
                        act(dst.t[0:C, :, 0:C], PS[bq][0:C, 0:2 * C].rearrange("p (a b) -> p a b", a=2), AF.Copy, [PSR[bq]], dst.R)
                    else:
                        act(dst.t[0:C, 1, 0:C], PS[bq][0:C, C:2 * C], AF.Copy, [PSR[bq]], dst.R)

                bq = squaring(cur, 1)
                yield
                evac_sq(nxt, bq, 1)
                yield
                for k in range(1, K + 1):
                    b5 = nps_p()
                    mm(PS[b5][0:C, 0:C], rr(nxt.t[0:C, 1, 0:C]), rr(d["PX"].t[0:C, 0:C]), True, True, nxt.R + d["PX"].R, [PSR[b5]])
                    if k < K:
                        bq = squaring(nxt, k + 1)
                    yield
                    dve_tt(d["PX"].t[0:C, 0:C], d["PX"].t[0:C, 0:C], PS[b5][0:C, 0:C], ALU.add, d["PX"].R + [PSR[b5]], d["PX"].R)
                    if k < K:
                        evac_sq(cur, bq, k + 1)
                    yield
                    cur, nxt = nxt, cur
                if not BF_INV:
                    act(d["NTB"].t[0:C, 0:C], d["PX"].t[0:C, 0:C], AF.Copy, d["PX"].R, d["NTB"].R)
                dve_tt(d["KD"].t[:, 0:C], ks, d["EE"].t[:, 0:C], ALU.mult, [kr] + d["EE"].R, d["KD"].R)
                dve_tt(d["QD"].t[:, 0:C], qs, d["EE"].t[:, 0:C], ALU.mult, [qr] + d["EE"].R, d["QD"].R)
                dve_ts(d["KW"].t[0:C, :], KTM.t[0:C, ci, h * 128:(h + 1) * 128], pcs["WJ"].t[0:C, h:h + 1], None, ALU.mult, None,
                       KTM.R + pcs["WJ"].R, d["KW"].R)
                yield


            def chain(ci, h):
                c0 = ci * C
                pcs = pc[ci % 2]
                d = dl[h % NSET]
                Sr, SBr = Sregs[h], SBregs[h]
                b1 = nps_c()
                mm(PS[b1][0:C, 0:128], d["KD"].t[:, 0:C], SBF.t[:, l, h, :], True, True, d["KD"].R + [SBr], [PSR[b1]])
                yield
                dve_tt(d["RP"].t[0:C, :], VTM.t[0:C, ci, h * 128:(h + 1) * 128], PS[b1][0:C, 0:128], ALU.subtract,
                       VTM.R + [PSR[b1]], d["RP"].R)
                yield
                b2 = nps_c()
                NTt = d["PX"] if BF_INV else d["NTB"]
                mm(PS[b2][0:C, 0:128], NTt.t[0:C, 0:C], d["RP"].t[0:C, :], True, True, NTt.R + d["RP"].R, [PSR[b2]])
                yield
                act(d["VNB"].t[0:C, :], PS[b2][0:C, 0:128], AF.Copy, [PSR[b2]] + BETA.R, d["VNB"].R, scale=BETA.t[0:C, ci, h:h + 1])
                yield
                b3 = nps_c()
                mm(PS[b3][:, 0:C], SBF.t[:, l, h, :], d["QD"].t[:, 0:C], True, False, [SBr] + d["QD"].R, [PSR[b3]])
                mm(PS[b3][:, 0:C], d["VNB"].t[0:C, :], d["ATT"].t[0:C, 0:C], False, True, d["VNB"].R + d["ATT"].R, [PSR[b3]])
                yield
                act(OT.t[:, h, c0:c0 + C], PS[b3][:, 0:C], AF.Copy, [PSR[b3]], [OT.r[h]])
                b4 = nps_c()
                mm(PS[b4][:, 0:128], d["KW"].t[0:C, :], d["VNB"].t[0:C, :], True, True, d["KW"].R + d["VNB"].R, [PSR[b4]])
                yield
                dve_stt(S32.t[:, l, h, :], S32.t[:, l, h, :], pcs["SDEC"].t[:, h:h + 1], PS[b4][:, 0:128], ALU.mult, ALU.add,
                        [Sr, PSR[b4]] + pcs["SDEC"].R, [Sr])
                yield
                act(SBF.t[:, l, h, :], S32.t[:, l, h, :], AF.Copy, [Sr], [SBr])
                yield


            def lockstep(gens):
                live = list(gens)
                while live:
                    nxt_live = []
                    for g in live:
                        try:
                            next(g)
                            nxt_live.append(g)
                        except StopIteration:
                            pass
                    live = nxt_live
                    for _ in range(NFILL):
                        mm(PS[7][:, 0:FILLN], ONEB.t[:], HB.t[:, 31, 0:FILLN], True, True, [ONEB.r[0]], [PSR[7]])


            groups = [(ci, h0) for ci in range(NCH) for h0 in range(0, 8, NSET)]
            if not OVL:
                for ci in range(NCH):
                    if ctx.sample:
                        dma("sp", S32.t[:, l, :, :], std_d[l, ci].rearrange("h k v -> k h v"), S32.dsem(l), [], Sregs)
                        for h in range(8):
                            act(SBF.t[:, l, h, :], S32.t[:, l, h, :], AF.Copy, [Sregs[h]], [SBregs[h]])
                    perchunk(ci)
                    for h0 in range(0, 8, NSET):
                        lockstep([pre(ci, h) for h in range(h0, h0 + NSET)])
                        lockstep([chain(ci, h) for h in range(h0, h0 + NSET)])
                    if ctx.sample or (ctx.last and ci == NCH - 1):
                        dma("sp", ctx.delta_out(l, ci).rearrange("h k v -> k h v"), S32.t[:, l, :, :], S32.dsem(l), Sregs, [], is_out=True)
                return
            perchunk(0)
            lockstep([pre(0, h) for h in range(0, NSET)])
            for gi, (ci, h0) in enumerate(groups):
                gens = [chain(ci, h) for h in range(h0, h0 + NSET)]
                if gi + 1 < len(groups):
                    cj, hj = groups[gi + 1]
                    if hj == 0:
                        perchunk(cj)
                    gens += [pre(cj, h) for h in range(hj, hj + NSET)]
                if h0 == 0 and ctx.sample:
                    dma("sp", S32.t[:, l, :, :], std_d[l, ci].rearrange("h k v -> k h v"), S32.dsem(l), [], Sregs)
                    for h in range(8):
                        act(SBF.t[:, l, h, :], S32.t[:, l, h, :], AF.Copy, [Sregs[h]], [SBregs[h]])
                lockstep(gens)
                if h0 + NSET >= 8 and (ctx.sample or (ctx.last and ci == NCH - 1)):
                    dma("sp", ctx.delta_out(l, ci).rearrange("h k v -> k h v"), S32.t[:, l, :, :], S32.dsem(l), Sregs, [], is_out=True)

        def layer(ctx, l):
            T, NS, L, C, NCH = ctx.T, ctx.NS, ctx.L, ctx.C, ctx.NCH
            EXT = NS * (16 + L)
            dma("pool", WBA.t[:], w_in_d[l, :, 7168:7184].rearrange("(k p) c -> p k c", p=128), WBA.dsem(), [], WBA.R)
            for blk in range(2):
                slot = w_next(f"up{blk}")

                def ev_pool(m, b, blk=blk):
                    kc = blk * 4 + m
                    g = kc // 2
                    w = 2 << g
                    ue = UE[kc % 2]
                    uv = ue.t[:, 0:EXT].rearrange("p (s e) -> p s e", s=NS)
                    dve_cp(uv[:, :, 1:16], PH.t[:, l, kc, 0:NS, 1:16], [PH.r[l]], ue.R)
                    act(uv[:, :, 16:16 + L], PS[b][:, 0:T].rearrange("p (s e) -> p s e", s=NS), AF.Copy, [PSR[b]], ue.R)
                    dve_cp(PH.t[:, l, kc, 0:NS, 1:16], uv[:, :, L + 1:L + 16], ue.R, [PH.r[l]])
                    src, srcR = ue.t, ue.R
                    sh = 1
                    pp = [SA, SB_]
                    i = 0
                    while sh < w:
                        dst = pp[i % 2]
                        dve_tt(dst.t[:, sh:EXT], src[:, sh:EXT], src[:, 0:EXT - sh], ALU.add, srcR, dst.R)
                        src, srcR = dst.t, dst.R
                        sh *= 2
                        i += 1
                    sv = src[:, 0:EXT].rearrange("p (s e) -> p s e", s=NS)
                    if ctx.first:
                        dve_tt(sv[:, 0, 16:32], sv[:, 0, 16:32], CST.t[:, C_CORR + g * 16:C_CORR + (g + 1) * 16], ALU.mult,
                               srcR + CST.R, srcR)
                    dve_stt(MIX.t[:, kc, 0:T].rearrange("p (s e) -> p s e", s=NS), sv[:, :, 16:16 + L], 1.0 / w,
                            uv[:, :, 16:16 + L], ALU.mult, ALU.subtract, srcR + ue.R, [MIX.r[kc]])

                proj(ctx, slot, 4, XB.t, XB.r, ev_pool)
            if STOP <= 1:
                return
            slot = w_next("wp")
            for dch in range(NK):
                g = dch // 2
                b = nps()
                for c in range(2):
                    mm(PS[b][:, 0:T], slot.t[:, g * 2 + c, (dch % 2) * 128:(dch % 2 + 1) * 128], MIX.t[:, g * 2 + c, 0:T],
                       c == 0, c == 1, [slot.r[0], MIX.r[g * 2 + c]], [PSR[b]])
                act(YA.t[:, dch, 0:T], PS[b][:, 0:T], AF.Copy, [PSR[b], VEC.r[0]], [YA.r[dch]], scale=vcol(l, V_PSC, dch))
            for blk in range(2):
                slot = w_next(f"ga{blk}")

                def ev_ga(m, b, blk=blk):
                    kc = blk * 4 + m
                    s = SCR[kc % 2]
                    act(s.t[:, 0:T], PS[b][:, 0:T], AF.Sigmoid, [PSR[b]], s.R)
                    dve_tt(YA.t[:, kc, 0:T], YA.t[:, kc, 0:T], s.t[:, 0:T], ALU.mult, [YA.r[kc]] + s.R, [YA.r[kc]])

                proj(ctx, slot, 4, XB.t, XB.r, ev_ga)
            if STOP <= 2:
                return
            P.fence(HB.R, QKV.R)
            pend = []
            for blk in range(6):
                slot = w_next(f"qkv{blk}")

                def ev_qkv(m, b, blk=blk):
                    mc = blk * 4 + m
                    ce = CE[mc % 2]
                    acc = ACC[mc % NACC]
                    cv = ce.t[:, 0:NS * (4 + L)].rearrange("p (s e) -> p s e", s=NS)
                    dve_cp(cv[:, :, 1:4], CH.t[:, l, mc, 0:NS, 1:4], [CH.r[l]], ce.R)
                    act(cv[:, :, 4:4 + L], PS[b][:, 0:T].rearrange("p (s e) -> p s e", s=NS), AF.Copy, [PSR[b]], ce.R)
                    dve_cp(CH.t[:, l, mc, 0:NS, 1:4], cv[:, :, L + 1:L + 4], ce.R, [CH.r[l]])
                    av = acc.t[:, 0:T].rearrange("p (s e) -> p s e", s=NS)
                    cw = lambda j: vcol(l, V_CW, j * 24 + mc)
                    dve_ts(av, cv[:, :, 1:1 + L], cw(0), None, ALU.mult, None, ce.R + VEC.R, acc.R)
                    for j in range(1, 4):
                        dve_stt(av, cv[:, :, 1 + j:1 + j + L], cw(j), av, ALU.mult, ALU.add, ce.R + acc.R + VEC.R, acc.R)
                    kind = mc // 8
                    hh = mc % 8
                    act(acc.t[:, 0:T], acc.t[:, 0:T], AF.Silu, acc.R, acc.R)
                    if kind == 2:
                        pend.append(lambda: to_tm(VTM, hh, acc.t, acc.R, ctx, VTM.R))
                        return
                    s = SCR[mc % NACC]
                    sq = s.t[:, 0:TT // 2].bitcast(BF16)[:, 0:T]
                    if POOLOFF:
                        P.add("pool", lambda h: h.tensor_tensor(out=sq, in0=acc.t[:, 0:T], in1=acc.t[:, 0:T], op=ALU.mult), acc.R, s.R)
                    else:
                        act(sq, acc.t[:, 0:T], AF.Square, acc.R, s.R)
                    pend.append(lambda: qk_post(kind, hh, acc, s, sq))

                def qk_post(kind, hh, acc, s, sq):
                    bb = nps()
                    mm(PS[bb][:, 0:T], ONEB.t[:], sq, True, True, [ONEB.r[0], s.r[0]], [PSR[bb]])
                    act(RS2.t[:, 0:T], PS[bb][:, 0:T], AF.Ln, [PSR[bb], EPSB.r[0]], RS2.R, bias=EPS_L2)
                    act(RS.t[:, 0:T], RS2.t[:, 0:T], AF.Exp, RS2.R, RS.R, scale=-0.5)
                    if kind == 0:
                        dve_stt(QT_t[:, hh, 0:T], acc.t[:, 0:T], 128.0 ** -0.5, RS.t[:, 0:T], ALU.mult, ALU.mult,
                                acc.R + RS.R, [QKV.r[hh]])
                    else:
                        dve_tt(acc.t[:, 0:T], acc.t[:, 0:T], RS.t[:, 0:T], ALU.mult, acc.R + RS.R, acc.R)
                        if POOLOFF:
                            pool_cp(KT_t[:, hh, 0:T], acc.t[:, 0:T], acc.R, [QKV.r[8 + hh]])
                        else:
                            act(KT_t[:, hh, 0:T], acc.t[:, 0:T], AF.Copy, acc.R, [QKV.r[8 + hh]])
                        to_tm(KTM, hh, acc.t, acc.R, ctx, KTM.R)

                def ev_qkv_d(m, b, blk=blk):
                    ev_qkv(m, b, blk)
                    if len(pend) >= NACC - 1:
                        pend.pop(0)()
                        pend.pop(0)()

                proj(ctx, slot, 4, XB.t, XB.r, ev_qkv_d)
            while pend:
                pend.pop(0)()
            if STOP <= 3:
                return
            for blk in range(2):
                slot = w_next(f"z{blk}")

                def ev_z(m, b, blk=blk):
                    kc = blk * 4 + m
                    act(GZ.t[:, kc, 0:T], PS[b][:, 0:T], AF.Silu, [PSR[b]], [GZ.r[kc]])

                proj(ctx, slot, 4, XB.t, XB.r, ev_z)
            for blk in range(2):
                slot = w_next(f"gb{blk}")

                def ev_gb(m, b, blk=blk):
                    kc = blk * 4 + m
                    s = SCR[kc % 2]
                    act(s.t[:, 0:T], PS[b][:, 0:T], AF.Sigmoid, [PSR[b]], s.R)
                    dve_tt(GZ.t[:, kc, 0:T], GZ.t[:, kc, 0:T], s.t[:, 0:T], ALU.mult, [GZ.r[kc]] + s.R, [GZ.r[kc]])

                proj(ctx, slot, 4, XB.t, XB.r, ev_gb)
            for ci in range(NCH):
                b = nps()
                for k in range(NK):
                    mm(PS[b][0:C, 0:16], XB.t[:, k, ci * C:(ci + 1) * C], WBA.t[:, k, :], k == 0, k == NK - 1,
                       [XB.r[k], WBA.r[0]], [PSR[b]])
                dve_tt(BAT.t[0:C, :], PS[b][0:C, 8:16], BC.t[0:C, l * 16 + 8:l * 16 + 16], ALU.add, [PSR[b]] + BC.R, BAT.R)
                act(BETA.t[0:C, ci, :], PS[b][0:C, 0:8], AF.Sigmoid, [PSR[b]] + BAT.R, BETA.R)
                act(BAT.t[0:C, :], BAT.t[0:C, :], AF.Exp, BAT.R, BAT.R)
                act(BAT.t[0:C, :], BAT.t[0:C, :], AF.Ln, BAT.R + EPSB.R, BAT.R, bias=ONE_B[0:C])
                dve_tt(GG.t[0:C, ci, :], BAT.t[0:C, :], NEGA.t[0:C, l * 8:(l + 1) * 8], ALU.mult, BAT.R + NEGA.R, GG.R)
            if STOP <= 4:
                return
            delta_phase(ctx, l)
            if STOP <= 5:
                return
            for h0 in range(0, NK, 4):
                bbs = {}
                for h in range(h0, h0 + 4):
                    s = SCR[h % NACC]
                    sq = s.t[:, 0:TT // 2].bitcast(BF16)[:, 0:T]
                    act(sq, OT.t[:, h, 0:T], AF.Square, [OT.r[h]], s.R)
                    bb = nps()
                    bbs[h] = bb
                    mm(PS[bb][:, 0:T], ONEB.t[:], sq, True, True, [ONEB.r[0], s.r[0]], [PSR[bb]])
                for h in range(h0, h0 + 4):
                    bb = bbs[h]
                    rs = RS if h % 2 == 0 else RS2
                    act(rs.t[:, 0:T], PS[bb][:, 0:T], AF.Ln, [PSR[bb], EPSB.r[0]], rs.R, bias=EPS_RMS, scale=1.0 / 128)
                    act(rs.t[:, 0:T], rs.t[:, 0:T], AF.Exp, rs.R, rs.R, scale=-0.5)
                    a = ACC[h % NACC]
                    dve_stt(a.t[:, 0:T], OT.t[:, h, 0:T], vcol(l, V_OG), rs.t[:, 0:T], ALU.mult, ALU.mult, [OT.r[h]] + rs.R + VEC.R, a.R)
                    dve_tt(a.t[:, 0:T], a.t[:, 0:T], GZ.t[:, h, 0:T], ALU.mult, a.R + [GZ.r[h]], a.R)
                    dve_tt(GZ.t[:, h, 0:T], a.t[:, 0:T], YA.t[:, h, 0:T], ALU.add, a.R + [YA.r[h]], [GZ.r[h]])
            for blk in range(2):
                slot = w_next(f"wo{blk}")

                def ev_wo(m, b, blk=blk):
                    kc = blk * 4 + m
                    dve_stt(X32.t[:, kc, 0:T], X32.t[:, kc, 0:T], ALPHA, PS[b][:, 0:T], ALU.mult, ALU.add, [X32.r[kc], PSR[b]], [X32.r[kc]])

                proj(ctx, slot, 4, GZ.t, GZ.r, ev_wo)
            ln_fm(ctx, l, V_LN1G, V_LN1B)
            if STOP <= 6:
                return
            P.fence(QKV.R, HB.R)
            for blk in range(8):
                slot = w_next(f"f1_{blk}")

                def ev_f1(m, b, blk=blk):
                    fc = blk * 4 + m
                    s = SCR[fc % 2]
                    act(s.t[:, 0:T], PS[b][:, 0:T], AF.Relu, [PSR[b], VEC.r[0]], s.R, bias=vcol(l, V_BF1, fc))
                    dve_tt(HB.t[:, fc, 0:T], s.t[:, 0:T], s.t[:, 0:T], ALU.mult, s.R, [HB.r[fc]])

                proj(ctx, slot, 4, XB.t, XB.r, ev_f1)
            for c in range(2):
                banks = [nps() for _ in range(4)]
                for kb in range(4):
                    slot = w_next(f"f2_{c}_{kb}")
                    for m in range(4):
                        for k in range(NK):
                            fc = kb * 8 + k
                            mm(PS[banks[m]][:, 0:T], slot.t[:, k, m * 128:(m + 1) * 128], HB.t[:, fc, 0:T],
                               kb == 0 and k == 0, kb == 3 and k == NK - 1, [slot.r[0], HB.r[fc]], [PSR[banks[m]]])
                for m in range(4):
                    kc = c * 4 + m
                    b = banks[m]
                    dve_stt(X32.t[:, kc, 0:T], X32.t[:, kc, 0:T], ALPHA, PS[b][:, 0:T], ALU.mult, ALU.add, [X32.r[kc], PSR[b]], [X32.r[kc]])
                    act(X32.t[:, kc, 0:T], X32.t[:, kc, 0:T], AF.Identity, [X32.r[kc], VEC.r[0]], [X32.r[kc]], bias=vcol(l, V_BF2, kc))
            ln_fm(ctx, l, V_LN2G, V_LN2B)

        def prompt_ctx(ti):
            c = Ctx()
            c.T, c.NS, c.L, c.C, c.NCH = TT, 1, TT, PC, TT // PC
            c.NRG, c.RGN = TT // 128, 128
            c.first = (ti == 0)
            c.last = (ti == NT - 1)
            c.x_d, c.y_d, c.tok0 = xp_d, yp_d, ti * TT
            c.pool_out = lambda l, s: pp_d[l]
            c.conv_out = lambda l, s: cp_d[l]
            c.delta_out = lambda l, s: dp_d[l]
            c.sample = False
            return c

        def sample_ctx():
            c = Ctx()
            c.T, c.NS, c.L, c.C, c.NCH = 64, 2, 32, 32, 2
            c.NRG, c.RGN = 1, 64
            c.first = False
            c.last = True
            c.x_d, c.y_d, c.tok0 = xs_d, ys_d, 0
            c.pool_out = lambda l, s: psm_d[l, s]
            c.conv_out = lambda l, s: csm_d[l, s]
            c.delta_out = lambda l, s: dsm_d[l, s]
            c.sample = True
            return c

        def run_sample():
            ctx = sample_ctx()
            load_input(ctx)
            for l in range(DEPTH):
                fm_state_load(ctx, l)
                if STOP >= 1:
                    layer(ctx, l)
                state_store(ctx, l)
            store_output(ctx)

        do_sample = with_sample and not SKIP_SAMPLE
        if do_sample and not SAMPLE_LAST:
            run_sample()
        for l in range(DEPTH):
            P.add("dve", lambda h, l=l: h.memset(PH.t[:, l], 0.0), [], [PH.r[l]])
            P.add("dve", lambda h, l=l: h.memset(CH.t[:, l], 0.0), [], [CH.r[l]])
            P.add("dve", lambda h, l=l: h.memset(S32.t[:, l], 0.0), [], [S32.r[l * 8 + h] for h in range(8)])
            P.add("dve", lambda h, l=l: h.memset(SBF.t[:, l], 0.0), [], [SBF.r[l * 8 + h] for h in range(8)])
        for ti in range(NT if not SKIP_PROMPT else 0):
            ctx = prompt_ctx(ti)
            load_input(ctx)
            for l in range(DEPTH):
                layer(ctx, l)
                if ctx.last:
                    state_store(ctx, l)
            store_output(ctx)
        if do_sample and SAMPLE_LAST:
            run_sample()
        info = P.emit(st)
        info["sbuf_bytes"] = Buf.total
        build_program.info = info
    return nc


def make_consts():
    c = np.zeros((128, NCST), np.float32)
    i = np.arange(128)
    c[:, C_ID:C_ID + 128] = np.eye(128, dtype=np.float32)
    c[:, C_TRI:C_TRI + 128] = (i[:, None] <= i[None, :]).astype(np.float32)
    c[:, C_NEGM:C_NEGM + 128] = np.where(i[None, :] >= i[:, None], 0.0, -30000.0).astype(np.float32)
    c[:, C_STR:C_STR + 128] = (i[None, :] > i[:, None]).astype(np.float32)
    c[:, C_ONE:C_ONE + 128] = 1.0
    for g in range(4):
        w = 2 << g
        t = np.arange(16)
        c[:, C_CORR + g * 16:C_CORR + (g + 1) * 16] = (w / np.minimum(w, t + 1)).astype(np.float32)[None, :]
    return c


def pack_vecs(inp):
    v = np.zeros((128, NV), np.float32)
    fm = lambda a: np.ascontiguousarray(np.asarray(a, np.float32).reshape(-1, 128).T)
    for l in range(DEPTH):
        o = l * LV
        v[:, o + V_LN1G:o + V_LN1G + 8] = fm(inp["ln1_g"][l])
        v[:, o + V_LN1B:o + V_LN1B + 8] = fm(inp["ln1_b"][l])
        v[:, o + V_LN2G:o + V_LN2G + 8] = fm(inp["ln2_g"][l])
        v[:, o + V_LN2B:o + V_LN2B + 8] = fm(inp["ln2_b"][l])
        v[:, o + V_BF2:o + V_BF2 + 8] = fm(inp["b_ff2"][l])
        v[:, o + V_PSC:o + V_PSC + 8] = fm(inp["pool_scale"][l])
        v[:, o + V_BF1:o + V_BF1 + 32] = fm(inp["b_ff1"][l])
        for j in range(4):
            v[:, o + V_CW + j * 24:o + V_CW + (j + 1) * 24] = fm(inp["conv_w"][l, j])
        v[:, o + V_OG] = np.asarray(inp["o_gain"][l], np.float32)
    v[:, V_ING:V_ING + 8] = fm(inp["ln_in_g"])
    v[:, V_INB:V_INB + 8] = fm(inp["ln_in_b"])
    return v


_CACHE = {}


def run(inp, SEQ=SEQ_FULL, with_sample=True, trace=False):
    key = (SEQ, with_sample)
    if key not in _CACHE:
        _CACHE[key] = build_program(SEQ, with_sample)
    nc = _CACHE[key]
    f = lambda a: np.ascontiguousarray(np.asarray(a, np.float32))
    cst = make_consts()
    vecs = pack_vecs(inp)
    bc = np.zeros((128, 32), np.float32)
    for l in range(DEPTH):
        bc[:, l * 16:l * 16 + 8] = np.asarray(inp["a_log"][l], np.float32)[None, :]
        bc[:, l * 16 + 8:l * 16 + 16] = np.asarray(inp["dt_bias"][l], np.float32)[None, :]
    shared = {"w_in": f(inp["w_in"]), "w_pool": f(inp["w_pool"]), "w_out": f(inp["w_out"]), "w_ff1": f(inp["w_ff1"]),
              "w_ff2": f(inp["w_ff2"]), "vecs": vecs, "bc": bc, "cst": cst}
    in_maps = []
    for c in range(8):
        m = dict(shared)
        m["xp"] = f(inp["x_prompt"][c, :SEQ])
        m["xs"] = f(inp["x_sample"][2 * c:2 * c + 2]).reshape(64, D)
        m["st_pool"] = f(inp["state_pool"][:, 2 * c:2 * c + 2])
        m["st_conv"] = f(inp["state_conv"][:, 2 * c:2 * c + 2])
        m["st_delta"] = f(inp["state_delta"][:, 2 * c:2 * c + 2])
        in_maps.append(m)
    res = run_bass_kernel_spmd(nc, in_maps, core_ids=list(range(8)), **({"trace": True} if trace else {}))
    R = res.results
    yp = np.stack([R[c]["yp"] for c in range(8)], 0)
    ys = np.concatenate([R[c]["ys"].reshape(2, 32, D) for c in range(8)], 0)
    pool_p = np.stack([R[c]["pool_p"] for c in range(8)], 1)
    conv_p = np.stack([R[c]["conv_p"] for c in range(8)], 1)
    delta_p = np.stack([R[c]["delta_p"] for c in range(8)], 1)
    pool_s = np.concatenate([R[c]["pool_s"] for c in range(8)], 1)
    conv_s = np.concatenate([R[c]["conv_s"] for c in range(8)], 1)
    delta_s = np.concatenate([R[c]["delta_s"] for c in range(8)], 1)
    outs = (yp, ys, pool_p, conv_p, delta_p, pool_s, conv_s, delta_s)
    return tuple(np.ascontiguousarray(o, dtype=np.float32) for o in outs), res


def kernel(**inputs):
    outs, _ = run(inputs)
    return outs
```

```python
import numpy as np
from contextlib import ExitStack
import concourse.bass as bass
import concourse.mybir as mybir
from concourse.bass_utils import run_bass_kernel_spmd

F32 = mybir.dt.float32
BF16 = mybir.dt.bfloat16
AF = mybir.ActivationFunctionType
ALU = mybir.AluOpType
SEM_LIM = 8000
import os
STOP = int(os.environ.get("KSTOP", "9"))
SKIP_PROMPT = int(os.environ.get("KSKIP_PROMPT", "0"))
SKIP_SAMPLE = int(os.environ.get("KSKIP_SAMPLE", "0"))
KDL = int(os.environ.get("KDL", "0"))
NSET = int(os.environ.get("KNSET", "4"))
WSCR = int(os.environ.get("KWSCR", "1"))
BF_INV = int(os.environ.get("KBFINV", "0"))
NACC = int(os.environ.get("KNACC", "4"))
WQ2 = int(os.environ.get("KWQ2", "1"))
POOLOFF = int(os.environ.get("KPOOLOFF", "1"))
OVL = int(os.environ.get("KOVL", "0"))
SAMPLE_LAST = int(os.environ.get("KSLAST", "1"))
NFILL = int(os.environ.get("KNFILL", "0"))
FILLN = int(os.environ.get("KFILLN", "256"))
FP32R = int(os.environ.get("KFP32R", "0"))
PC = int(os.environ.get("KPC", "128"))
TT = int(os.environ.get("KTT", "512"))

D = 1024
NK = 8
DEPTH = 2
SEQ_FULL = 4096
IN_W = 7184
ALPHA = (2 * DEPTH) ** 0.25
LN_EPS = 1e-5
RMS_EPS = 1e-6
L2_EPS = 1e-6
LV = 180
V_LN1G, V_LN1B, V_LN2G, V_LN2B, V_BF2, V_PSC, V_BF1, V_CW, V_OG = 0, 8, 16, 24, 32, 40, 48, 80, 176
V_ING, V_INB = 2 * LV, 2 * LV + 8
NV = 2 * LV + 16
C_ID, C_TRI, C_NEGM, C_STR, C_ONE, C_CORR = 0, 128, 256, 384, 512, 640
NCST = 640 + 64


class Reg:
    __slots__ = ("name", "w", "rs")

    def __init__(self, name):
        self.name = name
        self.w = None
        self.rs = []


class Ins:
    __slots__ = ("eng", "fn", "deps", "ticket", "need", "dsem", "dticket", "idx")


class DSem:
    __slots__ = ("name", "count", "sem")

    def __init__(self, name):
        self.name = name
        self.count = 0
        self.sem = None


class Prog:
    ENGS = ("pe", "act", "dve", "pool", "sp")

    def __init__(self, nc):
        self.nc = nc
        self.ins = []
        self.out_dmas = []
        self.dsems = []

    def dsem(self, name):
        d = DSem(name)
        self.dsems.append(d)
        return d

    def add(self, eng, fn, reads=(), writes=(), dsem=None, is_out=False):
        i = Ins()
        i.eng = eng
        i.fn = fn
        i.idx = len(self.ins)
        i.need = False
        i.ticket = None
        i.dsem = dsem
        i.dticket = None
        deps = set()
        for r in reads:
            if r.w is not None:
                deps.add(r.w)
        for w in writes:
            if w.w is not None:
                deps.add(w.w)
            deps.update(w.rs)
        for r in reads:
            if dsem is None:
                r.rs = [x for x in r.rs if not (self.ins[x].eng == eng and self.ins[x].dsem is None)]
            r.rs.append(i.idx)
        for w in writes:
            w.w = i.idx
            w.rs = []
        deps.discard(i.idx)
        i.deps = sorted(deps)
        if dsem is not None:
            dsem.count += 1
            i.dticket = dsem.count
        for d in i.deps:
            self.ins[d].need = True
        self.ins.append(i)
        if is_out:
            self.out_dmas.append(i.idx)
        return i

    def fence(self, src_regs, dst_regs):
        pend = []
        for s in src_regs:
            if s.w is not None:
                pend.append(s.w)
            pend.extend(s.rs)
        for d in dst_regs:
            d.rs = list(set(d.rs) | set(pend))

    def emit(self, stack):
        nc = self.nc
        fin = Ins()
        fin.eng = "sp"
        fin.fn = None
        fin.idx = len(self.ins)
        fin.need = False
        fin.dsem = None
        fin.deps = list(self.out_dmas)
        fin.ticket = None
        fin.dticket = None
        self.ins.append(fin)
        cnt = {e: 0 for e in self.ENGS}
        for i in self.ins:
            if i.dsem is None and i.need:
                cnt[i.eng] += 1
                i.ticket = cnt[i.eng]
        esems = {}
        for e in self.ENGS:
            n = (cnt[e] + SEM_LIM - 1) // SEM_LIM
            esems[e] = [stack.enter_context(nc.semaphore(f"s_{e}{k}")) for k in range(max(n, 1))]
        for d in self.dsems:
            if d.count > 0:
                d.sem = stack.enter_context(nc.semaphore(f"d_{d.name}"))
        per_eng = {e: [i for i in self.ins if i.eng == e] for e in self.ENGS}
        ins_all = self.ins
        nwaits = [0]

        def run_engine(e, h):
            wm = {}
            maxk = {}
            for i in per_eng[e]:
                for d in i.deps:
                    di = ins_all[d]
                    if di.dsem is not None:
                        key = ("d", id(di.dsem))
                        val = di.dticket * 16
                        sem = di.dsem.sem
                    else:
                        if di.eng == e and e == "pe":
                            continue
                        k = (di.ticket - 1) // SEM_LIM
                        if maxk.get(di.eng, -1) > k:
                            continue
                        key = (di.eng, k)
                        val = (di.ticket - 1) % SEM_LIM + 1
                        sem = esems[di.eng][k]
                    if wm.get(key, 0) >= val:
                        continue
                    h.wait_ge(sem, val)
                    nwaits[0] += 1
                    wm[key] = val
                    if di.dsem is None:
                        maxk[di.eng] = max(maxk.get(di.eng, -1), key[1])
                if i.fn is None:
                    continue
                r = i.fn(h)
                if i.dsem is not None:
                    r.then_inc(i.dsem.sem, 16)
                elif i.need:
                    k = (i.ticket - 1) // SEM_LIM
                    r.then_inc(esems[e][k], 1)

        block = stack.enter_context(nc.Block())

        @block.tensor
        def _(h):
            run_engine("pe", h)

        @block.scalar
        def _(h):
            run_engine("act", h)

        @block.vector
        def _(h):
            run_engine("dve", h)

        @block.gpsimd
        def _(h):
            run_engine("pool", h)

        @block.sync
        def _(h):
            run_engine("sp", h)

        return dict(n_ins=len(self.ins), n_waits=nwaits[0], cnt=cnt)


class Buf:
    total = 0
    sizes = []

    def __init__(self, P, st, nc, name, shape, dtype, nreg=1):
        self.t = st.enter_context(nc.sbuf_tensor(name, shape, dtype))
        nb = int(np.prod(shape[1:])) * (2 if dtype == BF16 else 4)
        Buf.total += (nb + 31) // 32 * 32
        Buf.sizes.append((name, nb))
        self.r = [Reg(f"{name}{i}") for i in range(nreg)]
        self.ds = [None] * nreg
        self.P = P
        self.name = name

    @property
    def R(self):
        return self.r

    def dsem(self, i=0):
        if self.ds[i] is None:
            self.ds[i] = self.P.dsem(f"{self.name}{i}")
        return self.ds[i]


def build_program(SEQ, with_sample=True):
    nc = bass.Bass("TRN2", target_bir_lowering=False)
    NT = SEQ // TT
    dr = lambda n, s, k: nc.dram_tensor(n, s, F32, kind=k).ap()
    xp_d = dr("xp", [SEQ, D], "ExternalInput")
    xs_d = dr("xs", [64, D], "ExternalInput")
    stp_d = dr("st_pool", [DEPTH, 2, 15, D], "ExternalInput")
    stc_d = dr("st_conv", [DEPTH, 2, 3, 3072], "ExternalInput")
    std_d = dr("st_delta", [DEPTH, 2, 8, 128, 128], "ExternalInput")
    w_in_d = dr("w_in", [DEPTH, D, IN_W], "ExternalInput")
    w_pool_d = dr("w_pool", [DEPTH, 4, 256, 256], "ExternalInput")
    w_out_d = dr("w_out", [DEPTH, D, D], "ExternalInput")
    w_ff1_d = dr("w_ff1", [DEPTH, D, 4096], "ExternalInput")
    w_ff2_d = dr("w_ff2", [DEPTH, 4096, D], "ExternalInput")
    vecs_d = dr("vecs", [128, NV], "ExternalInput")
    bc_d = dr("bc", [128, 32], "ExternalInput")
    cst_d = dr("cst", [128, NCST], "ExternalInput")
    yp_d = dr("yp", [SEQ, D], "ExternalOutput")
    ys_d = dr("ys", [64, D], "ExternalOutput")
    pp_d = dr("pool_p", [DEPTH, 15, D], "ExternalOutput")
    cp_d = dr("conv_p", [DEPTH, 3, 3072], "ExternalOutput")
    dp_d = dr("delta_p", [DEPTH, 8, 128, 128], "ExternalOutput")
    psm_d = dr("pool_s", [DEPTH, 2, 15, D], "ExternalOutput")
    csm_d = dr("conv_s", [DEPTH, 2, 3, 3072], "ExternalOutput")
    dsm_d = dr("delta_s", [DEPTH, 2, 8, 128, 128], "ExternalOutput")

    P = Prog(nc)
    st = ExitStack()
    with st:
        mk = lambda name, shape, dt=F32, nreg=1: Buf(P, st, nc, name, shape, dt, nreg)
        CST = mk("CST", [128, NCST])
        VEC = mk("VEC", [128, NV])
        BC = mk("BC", [128, 32])
        NEGA = mk("NEGA", [128, 16])
        ONEB = mk("ONEB", [128, 128], BF16)
        IDENT = CST.t[:, C_ID:C_ID + 128]
        TRI_B = mk("TRIB", [128, 128])
        TRI = TRI_B.t[:, :]
        ONES_B = mk("ONESB", [128, 128])
        NEGM = CST.t[:, C_NEGM:C_NEGM + 128]
        STRICT = CST.t[:, C_STR:C_STR + 128]
        ONES = ONES_B.t[:, :]
        ONESN = mk("ONESN", [128, 128])
        X32 = mk("X32", [128, NK, TT], F32, NK)
        XB = mk("XB", [128, NK, TT], BF16, NK)
        XIN = [mk(f"XIN{i}", [128, D]) for i in range(2)]
        STAT = mk("STAT", [128, 16])
        UE = [mk(f"UE{i}", [128, 16 + TT]) for i in range(2)]
        SA = mk("SA", [128, 16 + TT])
        SB_ = mk("SB", [128, 16 + TT])
        PH = mk("PH", [128, DEPTH, NK, 2, 16], F32, DEPTH)
        YA = mk("YA", [128, NK, TT], BF16, NK)
        SCR = [mk(f"SCR{i}", [128, TT]) for i in range(NACC)]
        CE = [mk(f"CE{i}", [128, 4 + TT]) for i in range(2)]
        ACC = [mk(f"ACC{i}", [128, TT]) for i in range(NACC)]
        CH = mk("CH", [128, DEPTH, 24, 2, 4], F32, DEPTH)
        HB = mk("HB", [128, 32, TT], BF16, 32)
        QT_t = HB.t[:, 0:8, :]
        KT_t = HB.t[:, 8:16, :]
        QKV = mk("QKVR", [1, 4], F32, 16)
        KTM = mk("KTM", [128, TT // PC, D], BF16, 1)
        VTM = mk("VTM", [128, TT // PC, D], BF16, 1)
        GZ = mk("GZ", [128, NK, TT], BF16, NK)
        MIX = GZ
        OT = XB
        RS = mk("RS", [128, TT])
        RS2 = mk("RS2", [128, TT])
        S32 = mk("S32", [128, DEPTH, 8, 128], F32, DEPTH * 8)
        SBF = mk("SBF", [128, DEPTH, 8, 128], BF16, DEPTH * 8)
        BETA = mk("BETA", [128, TT // PC, 8])
        GG = mk("GG", [128, TT // PC, 8])
        BAT = mk("BAT", [128, 8])
        WBA = mk("WBA", [128, NK, 16], BF16)
        STG = mk("STG", [16, D])
        STG2 = mk("STG2", [4, 512])
        NSLOT = 3
        WS = [mk(f"W{i}", [128, NK, 512], BF16) for i in range(NSLOT)]
        dl = []
        for s in range(NSET):
            d_ = {}
            for n in ("GH", "EE"):
                d_[n] = mk(f"{n}{s}", [128, PC])
            IDT = BF16 if BF_INV else (mybir.dt.float32r if FP32R else F32)
            d_["PX"] = mk(f"PX{s}", [128, PC], IDT)
            for n in ("AAa", "AAb"):
                d_[n] = mk(f"{n}{s}", [128, 2, PC], IDT)
            for n in ("ATT", "KD", "QD") + (() if BF_INV else ("NTB",)):
                d_[n] = mk(f"{n}{s}", [128, PC], BF16)
            for n in ("RP", "VNB", "KW"):
                d_[n] = mk(f"{n}{s}", [128, 128], BF16)
            dl.append(d_)
        pc = []
        for s in range(2):
            d_ = {}
            for n in ("GCC", "NGCC", "GL", "WJ", "SDEC", "TW"):
                d_[n] = mk(f"{n}{s}", [128, 8])
            pc.append(d_)
        PS = [st.enter_context(nc.psum_tensor(f"ps{i}", [128, 512], F32)) for i in range(8)]
        PSR = [Reg(f"ps{i}") for i in range(8)]
        psi = [0]

        def nps():
            b = psi[0] % (7 if NFILL else 8)
            psi[0] += 1
            return b

        def dve_tt(out, a, b, op, R, W):
            P.add("dve", lambda h: h.tensor_tensor(out=out, in0=a, in1=b, op=op), R, W)

        def dve_ts(out, a, s1, s2, op0, op1, R, W):
            if op1 is None:
                P.add("dve", lambda h: h.tensor_scalar(out=out, in0=a, scalar1=s1, scalar2=None, op0=op0), R, W)
            else:
                P.add("dve", lambda h: h.tensor_scalar(out=out, in0=a, scalar1=s1, scalar2=s2, op0=op0, op1=op1), R, W)

        def dve_stt(out, a, s, b, op0, op1, R, W):
            P.add("dve", lambda h: h.scalar_tensor_tensor(out=out, in0=a, scalar=s, in1=b, op0=op0, op1=op1), R, W)

        def pool_cp(out, a, R, W):
            P.add("pool", lambda h: h.tensor_copy(out=out, in_=a), R, W)

        def dve_cp(out, a, R, W):
            P.add("dve", lambda h: h.tensor_copy(out=out, in_=a), R, W)

        def act(out, a, func, R, W, bias=None, scale=None):
            kw = {}
            if bias is not None:
                kw["bias"] = bias
            if scale is not None:
                kw["scale"] = scale
            P.add("act", lambda h: h.activation(out=out, in_=a, func=func, **kw), R, W)

        def mm(out, lhsT, rhs, start, stop, R, W):
            P.add("pe", lambda h: h.matmul(out, lhsT, rhs, start=start, stop=stop), R, W)

        def tr(out, in_, n, R, W):
            P.add("pe", lambda h: h.transpose(out, in_, IDENT[0:n, 0:n]), R + [CST.r[0]], W)

        def dma(q, out, in_, ds, R, W, is_out=False):
            P.add(q, lambda h: h.dma_start(out=out, in_=in_), R, W, dsem=ds, is_out=is_out)

        def vcol(l, off, k=0):
            c = l * LV + off + k
            return VEC.t[:, c:c + 1]

        def layer_blocks(l):
            bl = []
            kp = lambda ap: ap.rearrange("(k p) c -> p k c", p=128)
            for i in range(2):
                bl.append((f"up{i}", kp(w_in_d[l, :, i * 512:(i + 1) * 512]), 512))
            bl.append(("wp", w_pool_d[l].rearrange("g (c p) d -> p (g c) d", p=128), 256))
            for i in range(2):
                bl.append((f"ga{i}", kp(w_in_d[l, :, 5120 + i * 512:5120 + (i + 1) * 512]), 512))
            for i in range(6):
                bl.append((f"qkv{i}", kp(w_in_d[l, :, 1024 + i * 512:1024 + (i + 1) * 512]), 512))
            for i in range(2):
                bl.append((f"z{i}", kp(w_in_d[l, :, 4096 + i * 512:4096 + (i + 1) * 512]), 512))
            for i in range(2):
                bl.append((f"gb{i}", kp(w_in_d[l, :, 6144 + i * 512:6144 + (i + 1) * 512]), 512))
            for i in range(2):
                bl.append((f"wo{i}", kp(w_out_d[l, :, i * 512:(i + 1) * 512]), 512))
            for i in range(8):
                bl.append((f"f1_{i}", kp(w_ff1_d[l, :, i * 512:(i + 1) * 512]), 512))
            for c in range(2):
                for k in range(4):
                    bl.append((f"f2_{c}_{k}", kp(w_ff2_d[l, k * 1024:(k + 1) * 1024, c * 512:(c + 1) * 512]), 512))
            return bl

        npass = NT + (1 if with_sample else 0)
        allblocks = []
        for _ in range(npass):
            for l in range(DEPTH):
                allblocks.extend(layer_blocks(l))
        wstate = dict(issued=0, used=0)

        NB = 2 * len(layer_blocks(0))
        wscr = nc.dram_tensor("wscr", [NB, 128, NK * 512], BF16, kind="Internal").ap()
        use_scr = (STOP >= 9 and not SKIP_PROMPT and not SKIP_SAMPLE and WSCR)

        def w_issue():
            i = wstate["issued"]
            if i >= len(allblocks):
                return
            name, src, ncol = allblocks[i]
            s = WS[i % NSLOT]
            j = i % NB
            scr = wscr[j, :, 0:NK * ncol].rearrange("p (k c) -> p k c", k=NK)
            if not hasattr(s, "hwds"):
                s.hwds = P.dsem(f"{s.name}hw")
            if use_scr and i >= NB:
                if WQ2 and i % 2 == 0:
                    dma("sp", s.t[:, :, 0:ncol], scr, s.hwds, [], s.R)
                else:
                    dma("pool", s.t[:, :, 0:ncol], scr, s.dsem(), [], s.R)
            else:
                dma("pool", s.t[:, :, 0:ncol], src, s.dsem(), [], s.R)
                if use_scr:
                    dma("sp", scr, s.t[:, :, 0:ncol], s.hwds, s.R, [])
            wstate["issued"] += 1

        def w_next(name):
            i = wstate["used"]
            if STOP < 9 or SKIP_PROMPT or SKIP_SAMPLE:
                while allblocks[i][0] != name:
                    del allblocks[i]
            assert allblocks[i][0] == name, (allblocks[i][0], name)
            while wstate["issued"] < min(i + NSLOT, len(allblocks)):
                w_issue()
            wstate["used"] += 1
            return WS[i % NSLOT]

        dma("sp", CST.t[:], cst_d, CST.dsem(), [], CST.R)
        dma("sp", VEC.t[:], vecs_d, VEC.dsem(), [], VEC.R)
        dma("sp", BC.t[:], bc_d, BC.dsem(), [], BC.R)
        for l in range(DEPTH):
            act(NEGA.t[:, l * 8:(l + 1) * 8], BC.t[:, l * 16:l * 16 + 8], AF.Exp, BC.R, NEGA.R)
        dve_ts(NEGA.t[:], NEGA.t[:], -1.0, None, ALU.mult, None, NEGA.R, NEGA.R)
        dve_cp(TRI_B.t[:], CST.t[:, C_TRI:C_TRI + 128], CST.R, CST.R)
        dve_cp(ONES_B.t[:], CST.t[:, C_ONE:C_ONE + 128], CST.R, CST.R)
        dve_cp(ONEB.t[:], ONES, CST.R, ONEB.R)
        dve_ts(ONESN.t[:], ONES, 1.0 / D, None, ALU.mult, None, CST.R, ONESN.R)

        for bf in UE + CE + [SA, SB_]:
            P.add("dve", lambda h, bf=bf: h.memset(bf.t[:], 0.0), [], bf.R)
        class Ctx:
            pass

        def ln_fm(ctx, l, goff, boff):
            T = ctx.T
            b = nps()
            for k in range(NK):
                mm(PS[b][:, 0:T], ONESN.t[:], X32.t[:, k, 0:T], k == 0, k == NK - 1, [ONESN.r[0], X32.r[k]], [PSR[b]])
            for k in range(NK):
                dve_tt(X32.t[:, k, 0:T], X32.t[:, k, 0:T], PS[b][:, 0:T], ALU.subtract, [X32.r[k], PSR[b]], [X32.r[k]])
            b2 = nps()
            for k in range(NK):
                s = SCR[k % 2]
                act(s.t[:, 0:T], X32.t[:, k, 0:T], AF.Square, [X32.r[k]], s.R)
                mm(PS[b2][:, 0:T], ONESN.t[:], s.t[:, 0:T], k == 0, k == NK - 1, [ONESN.r[0], s.r[0]], [PSR[b2]])
            act(RS2.t[:, 0:T], PS[b2][:, 0:T], AF.Ln, [PSR[b2], EPSB.r[0]], RS2.R, bias=EPS_LN)
            act(RS.t[:, 0:T], RS2.t[:, 0:T], AF.Exp, RS2.R, RS.R, scale=-0.5)
            for k in range(NK):
                dve_stt(X32.t[:, k, 0:T], X32.t[:, k, 0:T], vcol(l, goff, k) if l >= 0 else None, RS.t[:, 0:T],
                        ALU.mult, ALU.mult, [X32.r[k], RS.r[0], VEC.r[0]], [X32.r[k]])
                act(X32.t[:, k, 0:T], X32.t[:, k, 0:T], AF.Identity, [X32.r[k], VEC.r[0]], [X32.r[k]],
                    bias=vcol(l, boff, k))
                if POOLOFF:
                    pool_cp(XB.t[:, k, 0:T], X32.t[:, k, 0:T], [X32.r[k]], [XB.r[k]])
                else:
                    act(XB.t[:, k, 0:T], X32.t[:, k, 0:T], AF.Copy, [X32.r[k]], [XB.r[k]])

        def proj(ctx, slot, nchunk, rhs, rhs_regs, evac, coff=0):
            T = ctx.T
            for m in range(nchunk):
                b = nps()
                for k in range(NK):
                    mm(PS[b][:, 0:T], slot.t[:, k, coff + m * 128:coff + (m + 1) * 128], rhs[:, k, 0:T], k == 0, k == NK - 1,
                       [slot.r[0], rhs_regs[k]], [PSR[b]])
                evac(m, b)

        EPSB = mk("EPSB", [128, 4])
        P.add("dve", lambda h: h.memset(EPSB.t[:, 0:1], LN_EPS), [], EPSB.R)
        P.add("dve", lambda h: h.memset(EPSB.t[:, 1:2], RMS_EPS), [], EPSB.R)
        P.add("dve", lambda h: h.memset(EPSB.t[:, 2:3], L2_EPS), [], EPSB.R)
        P.add("dve", lambda h: h.memset(EPSB.t[:, 3:4], 1.0), [], EPSB.R)
        EPS_LN = EPSB.t[:, 0:1]
        EPS_RMS = EPSB.t[:, 1:2]
        EPS_L2 = EPSB.t[:, 2:3]
        ONE_B = EPSB.t[:, 3:4]

        def load_input(ctx):
            T = ctx.T
            for r in range(ctx.NRG):
                n = ctx.RGN
                xi = XIN[r % 2]
                dma("sp", xi.t[0:n, :], ctx.x_d[ctx.tok0 + r * n: ctx.tok0 + (r + 1) * n, :], xi.dsem(), [], xi.R)
                P.add("dve", lambda h, xi=xi, n=n: h.bn_stats(out=STAT.t[0:n, 0:6], in_=xi.t[0:n, 0:512]), xi.R, STAT.R)
                P.add("dve", lambda h, xi=xi, n=n: h.bn_stats(out=STAT.t[0:n, 6:12], in_=xi.t[0:n, 512:1024]), xi.R, STAT.R)
                P.add("dve", lambda h, n=n: h.bn_aggr(out=STAT.t[0:n, 12:14], in_=STAT.t[0:n, 0:12]), STAT.R, STAT.R)
                act(STAT.t[0:n, 14:15], STAT.t[0:n, 13:14], AF.Ln, STAT.R + EPSB.R, STAT.R, bias=EPS_LN[0:n])
                act(STAT.t[0:n, 15:16], STAT.t[0:n, 14:15], AF.Exp, STAT.R, STAT.R, scale=-0.5)
                dve_ts(xi.t[0:n, :], xi.t[0:n, :], STAT.t[0:n, 12:13], STAT.t[0:n, 15:16], ALU.subtract, ALU.mult,
                       xi.R + STAT.R, xi.R)
                for half in range(2):
                    b = nps()
                    for kk in range(4):
                        k = half * 4 + kk
                        tr(PS[b][:, kk * 128:kk * 128 + n], xi.t[0:n, k * 128:(k + 1) * 128], n, xi.R, [PSR[b]])
                    for kk in range(4):
                        k = half * 4 + kk
                        act(X32.t[:, k, r * n:(r + 1) * n], PS[b][:, kk * 128:kk * 128 + n], AF.Identity,
                            [PSR[b], VEC.r[0]], [X32.r[k]], bias=VEC.t[:, V_INB + k:V_INB + k + 1],
                            scale=VEC.t[:, V_ING + k:V_ING + k + 1])
                        dve_cp(XB.t[:, k, r * n:(r + 1) * n], X32.t[:, k, r * n:(r + 1) * n], [X32.r[k]], [XB.r[k]])

        def store_output(ctx):
            T = ctx.T
            for r in range(ctx.NRG):
                n = ctx.RGN
                xi = XIN[r % 2]
                for half in range(2):
                    b = nps()
                    for kk in range(4):
                        k = half * 4 + kk
                        tr(PS[b][0:n, kk * 128:(kk + 1) * 128], X32.t[:, k, r * n:(r + 1) * n], 128, [X32.r[k]], [PSR[b]])
                    act(xi.t[0:n, half * 512:(half + 1) * 512], PS[b][0:n, :], AF.Copy, [PSR[b]], xi.R)
                dma("sp", ctx.y_d[ctx.tok0 + r * n: ctx.tok0 + (r + 1) * n, :], xi.t[0:n, :], xi.dsem(), xi.R, [], is_out=True)

        def fm_state_load(ctx, l):
            for s in range(ctx.NS):
                dma("sp", STG.t[0:15, :], stp_d[l, s], STG.dsem(), [], STG.R)
                b = nps()
                for k in range(NK):
                    tr(PS[b][:, k * 16:k * 16 + 15], STG.t[0:15, k * 128:(k + 1) * 128], 15, STG.R, [PSR[b]])
                dve_cp(PH.t[:, l, :, s, 1:16], PS[b][:, 0:128].rearrange("p (k c) -> p k c", c=16)[:, :, 0:15], [PSR[b]], [PH.r[l]])
                for g in range(6):
                    dma("sp", STG2.t[0:3, :], stc_d[l, s, :, g * 512:(g + 1) * 512], STG2.dsem(), [], STG2.R)
                    b = nps()
                    for m in range(4):
                        tr(PS[b][:, m * 4:m * 4 + 3], STG2.t[0:3, m * 128:(m + 1) * 128], 3, STG2.R, [PSR[b]])
                    dve_cp(CH.t[:, l, g * 4:(g + 1) * 4, s, 1:4], PS[b][:, 0:16].rearrange("p (k c) -> p k c", c=4)[:, :, 0:3],
                           [PSR[b]], [CH.r[l]])

        def state_store(ctx, l):
            for s in range(ctx.NS):
                for half in range(2):
                    b = nps()
                    for kk in range(4):
                        k = half * 4 + kk
                        tr(PS[b][0:15, kk * 128:(kk + 1) * 128], PH.t[:, l, k, s, 1:16], 128, [PH.r[l]], [PSR[b]])
                    act(STG.t[0:15, half * 512:(half + 1) * 512], PS[b][0:15, :], AF.Copy, [PSR[b]], STG.R)
                dma("sp", ctx.pool_out(l, s), STG.t[0:15, :], STG.dsem(), STG.R, [], is_out=True)
                for g in range(6):
                    b = nps()
                    for mm_ in range(4):
                        m = g * 4 + mm_
                        tr(PS[b][0:3, mm_ * 128:(mm_ + 1) * 128], CH.t[:, l, m, s, 1:4], 128, [CH.r[l]], [PSR[b]])
                    act(STG2.t[0:3, :], PS[b][0:3, :], AF.Copy, [PSR[b]], STG2.R)
                    dma("sp", ctx.conv_out(l, s)[:, g * 512:(g + 1) * 512], STG2.t[0:3, :], STG2.dsem(), STG2.R, [], is_out=True)

        def to_tm(dst, hh, src_t, src_R, ctx, dst_regs):
            C, NCH = ctx.C, ctx.NCH
            for g0 in range(0, NCH, 4):
                n = min(4, NCH - g0)
                bb = nps()
                for j in range(n):
                    ci = g0 + j
                    tr(PS[bb][0:C, j * 128:(j + 1) * 128], src_t[:, ci * C:(ci + 1) * C], 128, src_R, [PSR[bb]])
                act(dst.t[0:C, g0:g0 + n, hh * 128:(hh + 1) * 128], PS[bb][0:C, 0:n * 128].rearrange("p (a b) -> p a b", b=128),
                    AF.Copy, [PSR[bb]], dst_regs)

        def delta_phase(ctx, l):
            T, NS, L, C, NCH = ctx.T, ctx.NS, ctx.L, ctx.C, ctx.NCH
            K = {128: 6, 64: 5, 32: 4}[C]
            Sregs = [S32.r[l * 8 + h] for h in range(8)]
            SBregs = [SBF.r[l * 8 + h] for h in range(8)]

            pcnt = [0, 0]

            def nps_c():
                if not OVL:
                    return nps()
                pcnt[0] += 1
                return (pcnt[0] - 1) % 4

            def nps_p():
                if not OVL:
                    return nps()
                pcnt[1] += 1
                return 4 + (pcnt[1] - 1) % 4

            def perchunk(ci):
                pcs = pc[ci % 2]
                b = nps()
                mm(PS[b][0:C, 0:8], TRI[0:C, 0:C], GG.t[0:C, ci, :], True, True, CST.R + GG.R, [PSR[b]])
                mm(PS[b][:, 8:16], ONES[0:C, :], GG.t[0:C, ci, :], True, True, CST.R + GG.R, [PSR[b]])
                dve_cp(pcs["GCC"].t[0:C, :], PS[b][0:C, 0:8], [PSR[b]], pcs["GCC"].R)
                dve_ts(pcs["NGCC"].t[0:C, :], PS[b][0:C, 0:8], -1.0, None, ALU.mult, None, [PSR[b]], pcs["NGCC"].R)
                dve_cp(pcs["GL"].t[:, :], PS[b][:, 8:16], [PSR[b]], pcs["GL"].R)
                dve_tt(pcs["TW"].t[0:C, :], pcs["GL"].t[0:C, :], pcs["GCC"].t[0:C, :], ALU.subtract,
                       pcs["GL"].R + pcs["GCC"].R, pcs["TW"].R)
                act(pcs["WJ"].t[0:C, :], pcs["TW"].t[0:C, :], AF.Exp, pcs["TW"].R, pcs["WJ"].R)
                act(pcs["SDEC"].t[:, :], pcs["GL"].t[:, :], AF.Exp, pcs["GL"].R, pcs["SDEC"].R)


            def pre(ci, h):
                c0 = ci * C
                pcs = pc[ci % 2]
                d = dl[h % NSET]
                G1 = d["GH"]
                qs = QT_t[:, h, c0:c0 + C]
                ks = KT_t[:, h, c0:c0 + C]
                qr, kr = QKV.r[h], QKV.r[8 + h]
                dve_ts(G1.t[0:C, 0:C], TRI[0:C, 0:C], GG.t[0:C, ci, h:h + 1], None, ALU.mult, None, CST.R + GG.R, G1.R)
                yield
                b1 = nps_p()
                mm(PS[b1][:, 0:C], ONES[0:C, :], G1.t[0:C, 0:C], True, True, CST.R + G1.R, [PSR[b1]])
                yield
                dve_tt(G1.t[0:C, 0:C], PS[b1][0:C, 0:C], NEGM[0:C, 0:C], ALU.add, [PSR[b1]] + CST.R, G1.R)
                yield
                act(d["EE"].t[:, 0:C], PS[b1][:, 0:C], AF.Exp, [PSR[b1]] + G1.R, d["EE"].R)
                act(G1.t[0:C, 0:C], G1.t[0:C, 0:C], AF.Exp, G1.R + pcs["NGCC"].R, G1.R, bias=pcs["NGCC"].t[0:C, h:h + 1])
                b2 = nps_p()
                mm(PS[b2][0:C, 0:C], ks, qs, True, True, [kr, qr], [PSR[b2]])
                mm(PS[b2][0:C, C:2 * C], ks, ks, True, True, [kr], [PSR[b2]])
                yield
                dve_tt(d["ATT"].t[0:C, 0:C], PS[b2][0:C, 0:C], G1.t[0:C, 0:C], ALU.mult, [PSR[b2]] + G1.R, d["ATT"].R)
                dve_stt(G1.t[0:C, 0:C], G1.t[0:C, 0:C], BETA.t[0:C, ci, h:h + 1], STRICT[0:C, 0:C], ALU.mult, ALU.mult,
                        G1.R + BETA.R + CST.R, G1.R)
                A, Bn = d["AAa"], d["AAb"]
                dve_tt(G1.t[0:C, 0:C], PS[b2][0:C, C:2 * C], G1.t[0:C, 0:C], ALU.mult, [PSR[b2]] + G1.R, G1.R)
                yield
                b3 = nps_p()
                tr(PS[b3][0:C, 0:C], G1.t[0:C, 0:C], C, G1.R, [PSR[b3]])
                dve_tt(d["PX"].t[0:C, 0:C], IDENT[0:C, 0:C], G1.t[0:C, 0:C], ALU.subtract, CST.R + G1.R, d["PX"].R)
                pool_cp(A.t[0:C, 0, 0:C], G1.t[0:C, 0:C], G1.R, A.R)
                yield
                act(A.t[0:C, 1, 0:C], PS[b3][0:C, 0:C], AF.Copy, [PSR[b3]], A.R)
                yield
                cur, nxt = A, Bn
                rr = (lambda ap: ap) if (not FP32R or C == 128) else (lambda ap: ap.bitcast(F32))

                def squaring(src, k):
                    bq = nps_p()
                    if k < K:
                        mm(PS[bq][0:C, 0:C], rr(src.t[0:C, 1, 0:C]), rr(src.t[0:C, 0, 0:C]), True, True, src.R, [PSR[bq]])
                    mm(PS[bq][0:C, C:2 * C], rr(src.t[0:C, 0, 0:C]), rr(src.t[0:C, 1, 0:C]), True, True, src.R, [PSR[bq]])
                    return bq

                def evac_sq(dst, bq, k):
                    if k < K:
                        act(dst.t[0:C, :, 0:C], PS[bq][0:C, 0:2 * C].rearrange("p (a b) -> p a b", a=2), AF.Copy, [PSR[bq]], dst.R)
                    else:
                        act(dst.t[0:C, 1, 0:C], PS[bq][0:C, C:2 * C], AF.Copy, [PSR[bq]], dst.R)

                bq = squaring(cur, 1)
                yield
                evac_sq(nxt, bq, 1)
                yield
                for k in range(1, K + 1):
                    b5 = nps_p()
                    mm(PS[b5][0:C, 0:C], rr(nxt.t[0:C, 1, 0:C]), rr(d["PX"].t[0:C, 0:C]), True, True, nxt.R + d["PX"].R, [PSR[b5]])
                    if k < K:
                        bq = squaring(nxt, k + 1)
                    yield
                    dve_tt(d["PX"].t[0:C, 0:C], d["PX"].t[0:C, 0:C], PS[b5][0:C, 0:C], ALU.add, d["PX"].R + [PSR[b5]], d["PX"].R)
                    if k < K:
                        evac_sq(cur, bq, k + 1)
                    yield
                    cur, nxt = nxt, cur
                if not BF_INV:
                    pool_cp(d["NTB"].t[0:C, 0:C], d["PX"].t[0:C, 0:C], d["PX"].R, d["NTB"].R)
                dve_tt(d["KD"].t[:, 0:C], ks, d["EE"].t[:, 0:C], ALU.mult, [kr] + d["EE"].R, d["KD"].R)
                dve_tt(d["QD"].t[:, 0:C], qs, d["EE"].t[:, 0:C], ALU.mult, [qr] + d["EE"].R, d["QD"].R)
                dve_ts(d["KW"].t[0:C, :], KTM.t[0:C, ci, h * 128:(h + 1) * 128], pcs["WJ"].t[0:C, h:h + 1], None, ALU.mult, None,
                       KTM.R + pcs["WJ"].R, d["KW"].R)
                yield


            def chain(ci, h):
                c0 = ci * C
                pcs = pc[ci % 2]
                d = dl[h % NSET]
                Sr, SBr = Sregs[h], SBregs[h]
                b1 = nps_c()
                mm(PS[b1][0:C, 0:128], d["KD"].t[:, 0:C], SBF.t[:, l, h, :], True, True, d["KD"].R + [SBr], [PSR[b1]])
                yield
                dve_tt(d["RP"].t[0:C, :], VTM.t[0:C, ci, h * 128:(h + 1) * 128], PS[b1][0:C, 0:128], ALU.subtract,
                       VTM.R + [PSR[b1]], d["RP"].R)
                yield
                b2 = nps_c()
                NTt = d["PX"] if BF_INV else d["NTB"]
                mm(PS[b2][0:C, 0:128], NTt.t[0:C, 0:C], d["RP"].t[0:C, :], True, True, NTt.R + d["RP"].R, [PSR[b2]])
                yield
                act(d["VNB"].t[0:C, :], PS[b2][0:C, 0:128], AF.Copy, [PSR[b2]] + BETA.R, d["VNB"].R, scale=BETA.t[0:C, ci, h:h + 1])
                yield
                b3 = nps_c()
                mm(PS[b3][:, 0:C], SBF.t[:, l, h, :], d["QD"].t[:, 0:C], True, False, [SBr] + d["QD"].R, [PSR[b3]])
                mm(PS[b3][:, 0:C], d["VNB"].t[0:C, :], d["ATT"].t[0:C, 0:C], False, True, d["VNB"].R + d["ATT"].R, [PSR[b3]])
                yield
                act(OT.t[:, h, c0:c0 + C], PS[b3][:, 0:C], AF.Copy, [PSR[b3]], [OT.r[h]])
                b4 = nps_c()
                mm(PS[b4][:, 0:128], d["KW"].t[0:C, :], d["VNB"].t[0:C, :], True, True, d["KW"].R + d["VNB"].R, [PSR[b4]])
                yield
                dve_stt(S32.t[:, l, h, :], S32.t[:, l, h, :], pcs["SDEC"].t[:, h:h + 1], PS[b4][:, 0:128], ALU.mult, ALU.add,
                        [Sr, PSR[b4]] + pcs["SDEC"].R, [Sr])
                yield
                pool_cp(SBF.t[:, l, h, :], S32.t[:, l, h, :], [Sr], [SBr])
                yield


            def lockstep(gens):
                live = list(gens)
                while live:
                    nxt_live = []
                    for g in live:
                        try:
                            next(g)
                            nxt_live.append(g)
                        except StopIteration:
                            pass
                    live = nxt_live
                    for _ in range(NFILL):
                        mm(PS[7][:, 0:FILLN], ONEB.t[:], HB.t[:, 31, 0:FILLN], True, True, [ONEB.r[0]], [PSR[7]])


            groups = [(ci, h0) for ci in range(NCH) for h0 in range(0, 8, NSET)]
            if not OVL:
                for ci in range(NCH):
                    if ctx.sample:
                        dma("sp", S32.t[:, l, :, :], std_d[l, ci].rearrange("h k v -> k h v"), S32.dsem(l), [], Sregs)
                        for h in range(8):
                            act(SBF.t[:, l, h, :], S32.t[:, l, h, :], AF.Copy, [Sregs[h]], [SBregs[h]])
                    perchunk(ci)
                    for h0 in range(0, 8, NSET):
                        lockstep([pre(ci, h) for h in range(h0, h0 + NSET)])
                        lockstep([chain(ci, h) for h in range(h0, h0 + NSET)])
                    if ctx.sample or (ctx.last and ci == NCH - 1):
                        dma("sp", ctx.delta_out(l, ci).rearrange("h k v -> k h v"), S32.t[:, l, :, :], S32.dsem(l), Sregs, [], is_out=True)
                return
            perchunk(0)
            lockstep([pre(0, h) for h in range(0, NSET)])
            for gi, (ci, h0) in enumerate(groups):
                gens = [chain(ci, h) for h in range(h0, h0 + NSET)]
                if gi + 1 < len(groups):
                    cj, hj = groups[gi + 1]
                    if hj == 0:
                        perchunk(cj)
                    gens += [pre(cj, h) for h in range(hj, hj + NSET)]
                if h0 == 0 and ctx.sample:
                    dma("sp", S32.t[:, l, :, :], std_d[l, ci].rearrange("h k v -> k h v"), S32.dsem(l), [], Sregs)
                    for h in range(8):
                        act(SBF.t[:, l, h, :], S32.t[:, l, h, :], AF.Copy, [Sregs[h]], [SBregs[h]])
                lockstep(gens)
                if h0 + NSET >= 8 and (ctx.sample or (ctx.last and ci == NCH - 1)):
                    dma("sp", ctx.delta_out(l, ci).rearrange("h k v -> k h v"), S32.t[:, l, :, :], S32.dsem(l), Sregs, [], is_out=True)

        def layer(ctx, l):
            T, NS, L, C, NCH = ctx.T, ctx.NS, ctx.L, ctx.C, ctx.NCH
            EXT = NS * (16 + L)
            dma("pool", WBA.t[:], w_in_d[l, :, 7168:7184].rearrange("(k p) c -> p k c", p=128), WBA.dsem(), [], WBA.R)
            for blk in range(2):
                slot = w_next(f"up{blk}")

                def ev_pool(m, b, blk=blk):
                    kc = blk * 4 + m
                    g = kc // 2
                    w = 2 << g
                    ue = UE[kc % 2]
                    uv = ue.t[:, 0:EXT].rearrange("p (s e) -> p s e", s=NS)
                    dve_cp(uv[:, :, 1:16], PH.t[:, l, kc, 0:NS, 1:16], [PH.r[l]], ue.R)
                    act(uv[:, :, 16:16 + L], PS[b][:, 0:T].rearrange("p (s e) -> p s e", s=NS), AF.Copy, [PSR[b]], ue.R)
                    dve_cp(PH.t[:, l, kc, 0:NS, 1:16], uv[:, :, L + 1:L + 16], ue.R, [PH.r[l]])
                    src, srcR = ue.t, ue.R
                    sh = 1
                    pp = [SA, SB_]
                    i = 0
                    while sh < w:
                        dst = pp[i % 2]
                        dve_tt(dst.t[:, sh:EXT], src[:, sh:EXT], src[:, 0:EXT - sh], ALU.add, srcR, dst.R)
                        src, srcR = dst.t, dst.R
                        sh *= 2
                        i += 1
                    sv = src[:, 0:EXT].rearrange("p (s e) -> p s e", s=NS)
                    if ctx.first:
                        dve_tt(sv[:, 0, 16:32], sv[:, 0, 16:32], CST.t[:, C_CORR + g * 16:C_CORR + (g + 1) * 16], ALU.mult,
                               srcR + CST.R, srcR)
                    dve_stt(MIX.t[:, kc, 0:T].rearrange("p (s e) -> p s e", s=NS), sv[:, :, 16:16 + L], 1.0 / w,
                            uv[:, :, 16:16 + L], ALU.mult, ALU.subtract, srcR + ue.R, [MIX.r[kc]])

                proj(ctx, slot, 4, XB.t, XB.r, ev_pool)
            if STOP <= 1:
                return
            slot = w_next("wp")
            for dch in range(NK):
                g = dch // 2
                b = nps()
                for c in range(2):
                    mm(PS[b][:, 0:T], slot.t[:, g * 2 + c, (dch % 2) * 128:(dch % 2 + 1) * 128], MIX.t[:, g * 2 + c, 0:T],
                       c == 0, c == 1, [slot.r[0], MIX.r[g * 2 + c]], [PSR[b]])
                act(YA.t[:, dch, 0:T], PS[b][:, 0:T], AF.Copy, [PSR[b], VEC.r[0]], [YA.r[dch]], scale=vcol(l, V_PSC, dch))
            for blk in range(2):
                slot = w_next(f"ga{blk}")

                def ev_ga(m, b, blk=blk):
                    kc = blk * 4 + m
                    s = SCR[kc % 2]
                    act(s.t[:, 0:T], PS[b][:, 0:T], AF.Sigmoid, [PSR[b]], s.R)
                    dve_tt(YA.t[:, kc, 0:T], YA.t[:, kc, 0:T], s.t[:, 0:T], ALU.mult, [YA.r[kc]] + s.R, [YA.r[kc]])

                proj(ctx, slot, 4, XB.t, XB.r, ev_ga)
            if STOP <= 2:
                return
            P.fence(HB.R, QKV.R)
            pend = []
            for blk in range(6):
                slot = w_next(f"qkv{blk}")

                def ev_qkv(m, b, blk=blk):
                    mc = blk * 4 + m
                    ce = CE[mc % 2]
                    acc = ACC[mc % NACC]
                    cv = ce.t[:, 0:NS * (4 + L)].rearrange("p (s e) -> p s e", s=NS)
                    dve_cp(cv[:, :, 1:4], CH.t[:, l, mc, 0:NS, 1:4], [CH.r[l]], ce.R)
                    act(cv[:, :, 4:4 + L], PS[b][:, 0:T].rearrange("p (s e) -> p s e", s=NS), AF.Copy, [PSR[b]], ce.R)
                    dve_cp(CH.t[:, l, mc, 0:NS, 1:4], cv[:, :, L + 1:L + 4], ce.R, [CH.r[l]])
                    av = acc.t[:, 0:T].rearrange("p (s e) -> p s e", s=NS)
                    cw = lambda j: vcol(l, V_CW, j * 24 + mc)
                    dve_ts(av, cv[:, :, 1:1 + L], cw(0), None, ALU.mult, None, ce.R + VEC.R, acc.R)
                    for j in range(1, 4):
                        dve_stt(av, cv[:, :, 1 + j:1 + j + L], cw(j), av, ALU.mult, ALU.add, ce.R + acc.R + VEC.R, acc.R)
                    kind = mc // 8
                    hh = mc % 8
                    act(acc.t[:, 0:T], acc.t[:, 0:T], AF.Silu, acc.R, acc.R)
                    if kind == 2:
                        pend.append(lambda: to_tm(VTM, hh, acc.t, acc.R, ctx, VTM.R))
                        return
                    s = SCR[mc % NACC]
                    sq = s.t[:, 0:TT // 2].bitcast(BF16)[:, 0:T]
                    if POOLOFF:
                        P.add("pool", lambda h: h.tensor_tensor(out=sq, in0=acc.t[:, 0:T], in1=acc.t[:, 0:T], op=ALU.mult), acc.R, s.R)
                    else:
                        act(sq, acc.t[:, 0:T], AF.Square, acc.R, s.R)
                    pend.append(lambda: qk_post(kind, hh, acc, s, sq))

                def qk_post(kind, hh, acc, s, sq):
                    bb = nps()
                    mm(PS[bb][:, 0:T], ONEB.t[:], sq, True, True, [ONEB.r[0], s.r[0]], [PSR[bb]])
                    act(RS2.t[:, 0:T], PS[bb][:, 0:T], AF.Ln, [PSR[bb], EPSB.r[0]], RS2.R, bias=EPS_L2)
                    act(RS.t[:, 0:T], RS2.t[:, 0:T], AF.Exp, RS2.R, RS.R, scale=-0.5)
                    if kind == 0:
                        dve_stt(QT_t[:, hh, 0:T], acc.t[:, 0:T], 128.0 ** -0.5, RS.t[:, 0:T], ALU.mult, ALU.mult,
                                acc.R + RS.R, [QKV.r[hh]])
                    else:
                        dve_tt(acc.t[:, 0:T], acc.t[:, 0:T], RS.t[:, 0:T], ALU.mult, acc.R + RS.R, acc.R)
                        if POOLOFF:
                            pool_cp(KT_t[:, hh, 0:T], acc.t[:, 0:T], acc.R, [QKV.r[8 + hh]])
                        else:
                            act(KT_t[:, hh, 0:T], acc.t[:, 0:T], AF.Copy, acc.R, [QKV.r[8 + hh]])
                        to_tm(KTM, hh, acc.t, acc.R, ctx, KTM.R)

                def ev_qkv_d(m, b, blk=blk):
                    ev_qkv(m, b, blk)
                    if len(pend) >= NACC - 1:
                        pend.pop(0)()
                        pend.pop(0)()

                proj(ctx, slot, 4, XB.t, XB.r, ev_qkv_d)
            while pend:
                pend.pop(0)()
            if STOP <= 3:
                return
            for blk in range(2):
                slot = w_next(f"z{blk}")

                def ev_z(m, b, blk=blk):
                    kc = blk * 4 + m
                    act(GZ.t[:, kc, 0:T], PS[b][:, 0:T], AF.Silu, [PSR[b]], [GZ.r[kc]])

                proj(ctx, slot, 4, XB.t, XB.r, ev_z)
            for blk in range(2):
                slot = w_next(f"gb{blk}")

                def ev_gb(m, b, blk=blk):
                    kc = blk * 4 + m
                    s = SCR[kc % 2]
                    act(s.t[:, 0:T], PS[b][:, 0:T], AF.Sigmoid, [PSR[b]], s.R)
                    dve_tt(GZ.t[:, kc, 0:T], GZ.t[:, kc, 0:T], s.t[:, 0:T], ALU.mult, [GZ.r[kc]] + s.R, [GZ.r[kc]])

                proj(ctx, slot, 4, XB.t, XB.r, ev_gb)
            for ci in range(NCH):
                b = nps()
                for k in range(NK):
                    mm(PS[b][0:C, 0:16], XB.t[:, k, ci * C:(ci + 1) * C], WBA.t[:, k, :], k == 0, k == NK - 1,
                       [XB.r[k], WBA.r[0]], [PSR[b]])
                dve_tt(BAT.t[0:C, :], PS[b][0:C, 8:16], BC.t[0:C, l * 16 + 8:l * 16 + 16], ALU.add, [PSR[b]] + BC.R, BAT.R)
                act(BETA.t[0:C, ci, :], PS[b][0:C, 0:8], AF.Sigmoid, [PSR[b]] + BAT.R, BETA.R)
                act(BAT.t[0:C, :], BAT.t[0:C, :], AF.Exp, BAT.R, BAT.R)
                act(BAT.t[0:C, :], BAT.t[0:C, :], AF.Ln, BAT.R + EPSB.R, BAT.R, bias=ONE_B[0:C])
                dve_tt(GG.t[0:C, ci, :], BAT.t[0:C, :], NEGA.t[0:C, l * 8:(l + 1) * 8], ALU.mult, BAT.R + NEGA.R, GG.R)
            if STOP <= 4:
                return
            delta_phase(ctx, l)
            if STOP <= 5:
                return
            for h0 in range(0, NK, 4):
                bbs = {}
                for h in range(h0, h0 + 4):
                    s = SCR[h % NACC]
                    sq = s.t[:, 0:TT // 2].bitcast(BF16)[:, 0:T]
                    act(sq, OT.t[:, h, 0:T], AF.Square, [OT.r[h]], s.R)
                    bb = nps()
                    bbs[h] = bb
                    mm(PS[bb][:, 0:T], ONEB.t[:], sq, True, True, [ONEB.r[0], s.r[0]], [PSR[bb]])
                for h in range(h0, h0 + 4):
                    bb = bbs[h]
                    rs = RS if h % 2 == 0 else RS2
                    act(rs.t[:, 0:T], PS[bb][:, 0:T], AF.Ln, [PSR[bb], EPSB.r[0]], rs.R, bias=EPS_RMS, scale=1.0 / 128)
                    act(rs.t[:, 0:T], rs.t[:, 0:T], AF.Exp, rs.R, rs.R, scale=-0.5)
                    a = ACC[h % NACC]
                    dve_stt(a.t[:, 0:T], OT.t[:, h, 0:T], vcol(l, V_OG), rs.t[:, 0:T], ALU.mult, ALU.mult, [OT.r[h]] + rs.R + VEC.R, a.R)
                    dve_tt(a.t[:, 0:T], a.t[:, 0:T], GZ.t[:, h, 0:T], ALU.mult, a.R + [GZ.r[h]], a.R)
                    dve_tt(GZ.t[:, h, 0:T], a.t[:, 0:T], YA.t[:, h, 0:T], ALU.add, a.R + [YA.r[h]], [GZ.r[h]])
            for blk in range(2):
                slot = w_next(f"wo{blk}")

                def ev_wo(m, b, blk=blk):
                    kc = blk * 4 + m
                    dve_stt(X32.t[:, kc, 0:T], X32.t[:, kc, 0:T], ALPHA, PS[b][:, 0:T], ALU.mult, ALU.add, [X32.r[kc], PSR[b]], [X32.r[kc]])

                proj(ctx, slot, 4, GZ.t, GZ.r, ev_wo)
            ln_fm(ctx, l, V_LN1G, V_LN1B)
            if STOP <= 6:
                return
            P.fence(QKV.R, HB.R)
            for blk in range(8):
                slot = w_next(f"f1_{blk}")

                def ev_f1(m, b, blk=blk):
                    fc = blk * 4 + m
                    s = SCR[fc % 2]
                    act(s.t[:, 0:T], PS[b][:, 0:T], AF.Relu, [PSR[b], VEC.r[0]], s.R, bias=vcol(l, V_BF1, fc))
                    dve_tt(HB.t[:, fc, 0:T], s.t[:, 0:T], s.t[:, 0:T], ALU.mult, s.R, [HB.r[fc]])

                proj(ctx, slot, 4, XB.t, XB.r, ev_f1)
            for c in range(2):
                banks = [nps() for _ in range(4)]
                for kb in range(4):
                    slot = w_next(f"f2_{c}_{kb}")
                    for m in range(4):
                        for k in range(NK):
                            fc = kb * 8 + k
                            mm(PS[banks[m]][:, 0:T], slot.t[:, k, m * 128:(m + 1) * 128], HB.t[:, fc, 0:T],
                               kb == 0 and k == 0, kb == 3 and k == NK - 1, [slot.r[0], HB.r[fc]], [PSR[banks[m]]])
                for m in range(4):
                    kc = c * 4 + m
                    b = banks[m]
                    dve_stt(X32.t[:, kc, 0:T], X32.t[:, kc, 0:T], ALPHA, PS[b][:, 0:T], ALU.mult, ALU.add, [X32.r[kc], PSR[b]], [X32.r[kc]])
                    act(X32.t[:, kc, 0:T], X32.t[:, kc, 0:T], AF.Identity, [X32.r[kc], VEC.r[0]], [X32.r[kc]], bias=vcol(l, V_BF2, kc))
            ln_fm(ctx, l, V_LN2G, V_LN2B)

        def prompt_ctx(ti):
            c = Ctx()
            c.T, c.NS, c.L, c.C, c.NCH = TT, 1, TT, PC, TT // PC
            c.NRG, c.RGN = TT // 128, 128
            c.first = (ti == 0)
            c.last = (ti == NT - 1)
            c.x_d, c.y_d, c.tok0 = xp_d, yp_d, ti * TT
            c.pool_out = lambda l, s: pp_d[l]
            c.conv_out = lambda l, s: cp_d[l]
            c.delta_out = lambda l, s: dp_d[l]
            c.sample = False
            return c

        def sample_ctx():
            c = Ctx()
            c.T, c.NS, c.L, c.C, c.NCH = 64, 2, 32, 32, 2
            c.NRG, c.RGN = 1, 64
            c.first = False
            c.last = True
            c.x_d, c.y_d, c.tok0 = xs_d, ys_d, 0
            c.pool_out = lambda l, s: psm_d[l, s]
            c.conv_out = lambda l, s: csm_d[l, s]
            c.delta_out = lambda l, s: dsm_d[l, s]
            c.sample = True
            return c

        def run_sample():
            ctx = sample_ctx()
            load_input(ctx)
            for l in range(DEPTH):
                fm_state_load(ctx, l)
                if STOP >= 1:
                    layer(ctx, l)
                state_store(ctx, l)
            store_output(ctx)

        do_sample = with_sample and not SKIP_SAMPLE
        if do_sample and not SAMPLE_LAST:
            run_sample()
        for l in range(DEPTH):
            P.add("dve", lambda h, l=l: h.memset(PH.t[:, l], 0.0), [], [PH.r[l]])
            P.add("dve", lambda h, l=l: h.memset(CH.t[:, l], 0.0), [], [CH.r[l]])
            P.add("dve", lambda h, l=l: h.memset(S32.t[:, l], 0.0), [], [S32.r[l * 8 + h] for h in range(8)])
            P.add("dve", lambda h, l=l: h.memset(SBF.t[:, l], 0.0), [], [SBF.r[l * 8 + h] for h in range(8)])
        for ti in range(NT if not SKIP_PROMPT else 0):
            ctx = prompt_ctx(ti)
            load_input(ctx)
            for l in range(DEPTH):
                layer(ctx, l)
                if ctx.last:
                    state_store(ctx, l)
            store_output(ctx)
        if do_sample and SAMPLE_LAST:
            run_sample()
        info = P.emit(st)
        info["sbuf_bytes"] = Buf.total
        build_program.info = info
    return nc


def make_consts():
    c = np.zeros((128, NCST), np.float32)
    i = np.arange(128)
    c[:, C_ID:C_ID + 128] = np.eye(128, dtype=np.float32)
    c[:, C_TRI:C_TRI + 128] = (i[:, None] <= i[None, :]).astype(np.float32)
    c[:, C_NEGM:C_NEGM + 128] = np.where(i[None, :] >= i[:, None], 0.0, -30000.0).astype(np.float32)
    c[:, C_STR:C_STR + 128] = (i[None, :] > i[:, None]).astype(np.float32)
    c[:, C_ONE:C_ONE + 128] = 1.0
    for g in range(4):
        w = 2 << g
        t = np.arange(16)
        c[:, C_CORR + g * 16:C_CORR + (g + 1) * 16] = (w / np.minimum(w, t + 1)).astype(np.float32)[None, :]
    return c


def pack_vecs(inp):
    v = np.zeros((128, NV), np.float32)
    fm = lambda a: np.ascontiguousarray(np.asarray(a, np.float32).reshape(-1, 128).T)
    for l in range(DEPTH):
        o = l * LV
        v[:, o + V_LN1G:o + V_LN1G + 8] = fm(inp["ln1_g"][l])
        v[:, o + V_LN1B:o + V_LN1B + 8] = fm(inp["ln1_b"][l])
        v[:, o + V_LN2G:o + V_LN2G + 8] = fm(inp["ln2_g"][l])
        v[:, o + V_LN2B:o + V_LN2B + 8] = fm(inp["ln2_b"][l])
        v[:, o + V_BF2:o + V_BF2 + 8] = fm(inp["b_ff2"][l])
        v[:, o + V_PSC:o + V_PSC + 8] = fm(inp["pool_scale"][l])
        v[:, o + V_BF1:o + V_BF1 + 32] = fm(inp["b_ff1"][l])
        for j in range(4):
            v[:, o + V_CW + j * 24:o + V_CW + (j + 1) * 24] = fm(inp["conv_w"][l, j])
        v[:, o + V_OG] = np.asarray(inp["o_gain"][l], np.float32)
    v[:, V_ING:V_ING + 8] = fm(inp["ln_in_g"])
    v[:, V_INB:V_INB + 8] = fm(inp["ln_in_b"])
    return v


_CACHE = {}


def run(inp, SEQ=SEQ_FULL, with_sample=True, trace=False):
    key = (SEQ, with_sample)
    if key not in _CACHE:
        _CACHE[key] = build_program(SEQ, with_sample)
    nc = _CACHE[key]
    f = lambda a: np.ascontiguousarray(np.asarray(a, np.float32))
    cst = make_consts()
    vecs = pack_vecs(inp)
    bc = np.zeros((128, 32), np.float32)
    for l in range(DEPTH):
        bc[:, l * 16:l * 16 + 8] = np.asarray(inp["a_log"][l], np.float32)[None, :]
        bc[:, l * 16 + 8:l * 16 + 16] = np.asarray(inp["dt_bias"][l], np.float32)[None, :]
    shared = {"w_in": f(inp["w_in"]), "w_pool": f(inp["w_pool"]), "w_out": f(inp["w_out"]), "w_ff1": f(inp["w_ff1"]),
              "w_ff2": f(inp["w_ff2"]), "vecs": vecs, "bc": bc, "cst": cst}
    in_maps = []
    for c in range(8):
        m = dict(shared)
        m["xp"] = f(inp["x_prompt"][c, :SEQ])
        m["xs"] = f(inp["x_sample"][2 * c:2 * c + 2]).reshape(64, D)
        m["st_pool"] = f(inp["state_pool"][:, 2 * c:2 * c + 2])
        m["st_conv"] = f(inp["state_conv"][:, 2 * c:2 * c + 2])
        m["st_delta"] = f(inp["state_delta"][:, 2 * c:2 * c + 2])
        in_maps.append(m)
    res = run_bass_kernel_spmd(nc, in_maps, core_ids=list(range(8)), **({"trace": True} if trace else {}))
    R = res.results
    yp = np.stack([R[c]["yp"] for c in range(8)], 0)
    ys = np.concatenate([R[c]["ys"].reshape(2, 32, D) for c in range(8)], 0)
    pool_p = np.stack([R[c]["pool_p"] for c in range(8)], 1)
    conv_p = np.stack([R[c]["conv_p"] for c in range(8)], 1)
    delta_p = np.stack([R[c]["delta_p"] for c in range(8)], 1)
    pool_s = np.concatenate([R[c]["pool_s"] for c in range(8)], 1)
    conv_s = np.concatenate([R[c]["conv_s"] for c in range(8)], 1)
    delta_s = np.concatenate([R[c]["delta_s"] for c in range(8)], 1)
    outs = (yp, ys, pool_p, conv_p, delta_p, pool_s, conv_s, delta_s)
    return tuple(np.ascontiguousarray(o, dtype=np.float32) for o in outs), res


def kernel(**inputs):
    outs, _ = run(inputs)
    return outs
```
